# Optimizing a Trainium2 kernel written in Bass

```python
import math
import jax, jax.numpy as jnp
from jax import lax
import numpy as np

D_MODEL = 1024
BATCH = 8
SEQ = 4096
DEPTH = 1

MEM_LEN = 256
HEAD_DIM = 64
LRU_WIDTH = 512
LRU_BLOCKS = 8
LRU_BLOCK = LRU_WIDTH // LRU_BLOCKS
CONV_WIDTH = 4
LRU_C = 8.0
SWA_Q_HEADS = 4
SWA_KV_HEADS = 2
SWA_GROUP = SWA_Q_HEADS // SWA_KV_HEADS
SWA_WIDTH = SWA_Q_HEADS * HEAD_DIM
SWA_KV_WIDTH = SWA_KV_HEADS * HEAD_DIM
WINDOW = 128
BLOCK = 128
XATTN_HEADS = 4
XATTN_WIDTH = XATTN_HEADS * HEAD_DIM
D_MIX = LRU_WIDTH + SWA_WIDTH + XATTN_WIDTH
IN_SPLITS = (LRU_WIDTH, LRU_WIDTH, SWA_WIDTH, SWA_KV_WIDTH, SWA_KV_WIDTH, SWA_WIDTH, XATTN_WIDTH, XATTN_WIDTH)
D_IN = sum(IN_SPLITS)
ROPE_THETA = 500000.0
ROPE_DIM = HEAD_DIM // 4
EPS = 1e-6
NEG_INF = -1e30

kernel_name = "hymba_style_rglru_swa_sink_memxattn"


def _rmsnorm(x, g):
    xf = x.astype(jnp.float32)
    y = xf * lax.rsqrt(jnp.mean(xf * xf, axis=-1, keepdims=True) + EPS)
    return (y * g.astype(jnp.float32)).astype(x.dtype)


def _rope_tables(seq):
    pos = jnp.arange(seq, dtype=jnp.float32)
    inv_freq = ROPE_THETA ** (-(jnp.arange(0, ROPE_DIM, 2, dtype=jnp.float32) / ROPE_DIM))
    ang = pos[:, None] * inv_freq[None, :]
    return jnp.cos(ang), jnp.sin(ang)


def _partial_rope(t, cos, sin):
    tf = t.astype(jnp.float32)
    half = ROPE_DIM // 2
    x1, x2, rest = tf[..., :half], tf[..., half:ROPE_DIM], tf[..., ROPE_DIM:]
    c, s = cos[None, :, None, :], sin[None, :, None, :]
    return jnp.concatenate([x1 * c - x2 * s, x2 * c + x1 * s, rest], axis=-1).astype(t.dtype)


def _rg_lru(u, conv_w, conv_b, w_rg, b_rg, w_ig, b_ig, lam):
    B, S, W = u.shape
    xc = lax.conv_general_dilated(
        u, conv_w[:, None, :].astype(u.dtype), window_strides=(1,),
        padding=[(CONV_WIDTH - 1, 0)], dimension_numbers=("NWC", "WIO", "NWC"),
        feature_group_count=W) + conv_b.astype(u.dtype)
    xf = xc.astype(jnp.float32)
    xblk = xf.reshape(B, S, LRU_BLOCKS, LRU_BLOCK)
    r = jax.nn.sigmoid(jnp.einsum("bsnd,nde->bsne", xblk, w_rg.astype(jnp.float32)).reshape(B, S, W)
                       + b_rg.astype(jnp.float32))
    i = jax.nn.sigmoid(jnp.einsum("bsnd,nde->bsne", xblk, w_ig.astype(jnp.float32)).reshape(B, S, W)
                       + b_ig.astype(jnp.float32))
    log_a = -LRU_C * r * jax.nn.softplus(-lam.astype(jnp.float32))
    a = jnp.exp(log_a)
    b = jnp.sqrt(-jnp.expm1(2.0 * log_a)) * (i * xf)

    def combine(c1, c2):
        a1, b1 = c1
        a2, b2 = c2
        return a1 * a2, a2 * b1 + b2

    _, h = lax.associative_scan(combine, (a, b), axis=1)
    return h.astype(u.dtype)


def _band(t):
    B, S, H, D = t.shape
    tb = t.reshape(B, S // BLOCK, BLOCK, H, D)
    prev = jnp.pad(tb, ((0, 0), (1, 0), (0, 0), (0, 0), (0, 0)))[:, :-1]
    return jnp.concatenate([prev, tb], axis=2)


def _swa_sink_attention(q, k, v, sinks):
    B, S, Hq, D = q.shape
    nb = S // BLOCK
    qb = q.reshape(B, nb, BLOCK, SWA_KV_HEADS, SWA_GROUP, D)
    kb, vb = _band(k), _band(v)
    s = jnp.einsum("bnqhgd,bnkhd->bnhgqk", qb, kb,
                   preferred_element_type=jnp.float32) * (1.0 / math.sqrt(D))
    qi = jnp.arange(BLOCK)[:, None]
    kj = jnp.arange(2 * BLOCK)[None, :]
    rel = qi + BLOCK - kj
    band_mask = (rel >= 0) & (rel < WINDOW)
    blk = jnp.arange(nb)[:, None, None]
    mask = band_mask[None] & ((blk > 0) | (kj >= BLOCK)[None])
    s = jnp.where(mask[None, :, None, None], s, NEG_INF)
    sink = sinks.astype(jnp.float32).reshape(SWA_KV_HEADS, SWA_GROUP)[None, None, :, :, None, None]
    m = jnp.maximum(jnp.max(s, axis=-1, keepdims=True), sink)
    p = jnp.exp(s - m)
    denom = jnp.sum(p, axis=-1, keepdims=True) + jnp.exp(sink - m)
    o = jnp.einsum("bnhgqk,bnkhd->bnqhgd", (p / denom).astype(v.dtype), vb)
    return o.reshape(B, S, Hq * D)


def _memory_attention(q, km, vm):
    B, S, H, D = q.shape
    s = jnp.einsum("bshd,bmhd->bhsm", q, km, preferred_element_type=jnp.float32) * (1.0 / math.sqrt(D))
    p = jax.nn.softmax(s, axis=-1)
    o = jnp.einsum("bhsm,bmhd->bshd", p.astype(vm.dtype), vm)
    return o.reshape(B, S, H * D)


def setup_inputs(seed: int = 0) -> dict:
    key = jax.random.key(seed)
    ks = jax.random.split(key, 20)
    f32 = jnp.float32
    nrm = lambda k, shape, scale: jax.random.normal(k, shape, f32) * scale
    x = jax.random.normal(ks[0], (BATCH, SEQ, D_MODEL), f32)
    mem = jax.random.normal(ks[1], (BATCH, MEM_LEN, D_MODEL), f32)
    u = jax.random.uniform(ks[11], (DEPTH, LRU_WIDTH), f32, 0.9, 0.999) ** (1.0 / LRU_C)
    lru_lambda = jnp.log(u) - jnp.log1p(-u)
    return {
        "x": x,
        "mem": mem,
        "norm_g": 1.0 + nrm(ks[2], (DEPTH, D_MODEL), 0.02),
        "mem_norm_g": 1.0 + nrm(ks[3], (DEPTH, D_MODEL), 0.02),
        "w_in": nrm(ks[4], (DEPTH, D_MODEL, D_IN), D_MODEL ** -0.5),
        "conv_w": nrm(ks[5], (DEPTH, CONV_WIDTH, LRU_WIDTH), CONV_WIDTH ** -0.5),
        "conv_b": nrm(ks[6], (DEPTH, LRU_WIDTH), 0.01),
        "w_rg": nrm(ks[7], (DEPTH, LRU_BLOCKS, LRU_BLOCK, LRU_BLOCK), LRU_BLOCK ** -0.5),
        "b_rg": nrm(ks[8], (DEPTH, LRU_WIDTH), 0.01),
        "w_ig": nrm(ks[9], (DEPTH, LRU_BLOCKS, LRU_BLOCK, LRU_BLOCK), LRU_BLOCK ** -0.5),
        "b_ig": nrm(ks[10], (DEPTH, LRU_WIDTH), 0.01),
        "lru_lambda": lru_lambda,
        "q_norm_g": 1.0 + nrm(ks[12], (DEPTH, HEAD_DIM), 0.02),
        "k_norm_g": 1.0 + nrm(ks[13], (DEPTH, HEAD_DIM), 0.02),
        "sinks": nrm(ks[14], (DEPTH, SWA_Q_HEADS), 0.5),
        "w_mem_kv": nrm(ks[15], (DEPTH, D_MODEL, 2 * XATTN_WIDTH), D_MODEL ** -0.5),
        "xq_norm_g": 1.0 + nrm(ks[16], (DEPTH, HEAD_DIM), 0.02),
        "xk_norm_g": 1.0 + nrm(ks[17], (DEPTH, HEAD_DIM), 0.02),
        "out_norm_g": 1.0 + nrm(ks[18], (DEPTH, D_MIX), 0.02),
        "w_out": nrm(ks[19], (DEPTH, D_MIX, D_MODEL), D_MIX ** -0.5),
    }


def reference(x, mem, norm_g, mem_norm_g, w_in, conv_w, conv_b, w_rg, b_rg, w_ig, b_ig, lru_lambda,
              q_norm_g, k_norm_g, sinks, w_mem_kv, xq_norm_g, xk_norm_g, out_norm_g, w_out):
    B, S, _ = x.shape
    M = mem.shape[1]
    cos, sin = _rope_tables(S)
    split_idx = list(np.cumsum(IN_SPLITS)[:-1])
    out_idx = [LRU_WIDTH, LRU_WIDTH + SWA_WIDTH]
    h = x
    for l in range(DEPTH):
        xn = _rmsnorm(h, norm_g[l])
        proj = jnp.einsum("bsd,de->bse", xn, w_in[l])
        (lru_x, lru_gate, sq, sk, sv, swa_gate, xq, x_gate) = jnp.split(proj, split_idx, axis=-1)

        y_a = _rg_lru(lru_x, conv_w[l], conv_b[l], w_rg[l], b_rg[l], w_ig[l], b_ig[l], lru_lambda[l])

        q = _partial_rope(_rmsnorm(sq.reshape(B, S, SWA_Q_HEADS, HEAD_DIM), q_norm_g[l]), cos, sin)
        k = _partial_rope(_rmsnorm(sk.reshape(B, S, SWA_KV_HEADS, HEAD_DIM), k_norm_g[l]), cos, sin)
        v = sv.reshape(B, S, SWA_KV_HEADS, HEAD_DIM)
        y_b = _swa_sink_attention(q, k, v, sinks[l])

        mn = _rmsnorm(mem, mem_norm_g[l])
        mkv = jnp.einsum("bmd,de->bme", mn, w_mem_kv[l])
        km, vm = jnp.split(mkv, 2, axis=-1)
        km = _rmsnorm(km.reshape(B, M, XATTN_HEADS, HEAD_DIM), xk_norm_g[l])
        vm = vm.reshape(B, M, XATTN_HEADS, HEAD_DIM)
        qx = _rmsnorm(xq.reshape(B, S, XATTN_HEADS, HEAD_DIM), xq_norm_g[l])
        y_c = _memory_attention(qx, km, vm)

        g_a, g_b, g_c = jnp.split(out_norm_g[l], out_idx)
        y = jnp.concatenate([
            _rmsnorm(y_a, g_a) * jax.nn.silu(lru_gate),
            _rmsnorm(y_b, g_b) * jax.nn.silu(swa_gate),
            _rmsnorm(y_c, g_c) * jax.nn.silu(x_gate),
        ], axis=-1)
        h = h + jnp.einsum("bse,ed->bsd", y, w_out[l]).astype(h.dtype)
    return h
```

```python
import math
from contextlib import ExitStack

import numpy as np
import concourse.bass as bass
import concourse.mybir as mybir
from concourse.bass_utils import run_bass_kernel_spmd

F32 = mybir.dt.float32
BF16 = mybir.dt.bfloat16
ALU = mybir.AluOpType
AF = mybir.ActivationFunctionType

S = 4096
D = 1024
DIN = 2304
NT = 512
NCH = S // NT
EPS = 1e-6
MEM = 256
STRICT_SAME_ENGINE = True

V_NORMG, V_MEMG, V_OUTG, V_CONVW, V_CONVB, V_BRG, V_BIG, V_LAM = 0, 8, 16, 24, 40, 44, 48, 52
V_QG, V_KG, V_XQG, V_XKG, V_SINK = 56, 57, 58, 59, 60
NV = 64
DV_HBRG, DV_HBIG, DV_CL, DV_HCL, DV_ESINK, DV_OUTG, DV_TMP = 0, 4, 8, 12, 16, 18, 26
NDV = 32


class Rec:
    def __init__(self):
        self.ops = []
        self.last_w = {}
        self.readers = {}
        self.group_keys = set()

    def add(self, eng, emit, reads=(), writes=(), dma_key=None):
        i = len(self.ops)
        deps = {}
        for k in reads:
            w = self.last_w.get(k)
            if w is not None:
                deps[w] = "raw"
        for k in writes:
            w = self.last_w.get(k)
            if w is not None and w not in deps:
                deps[w] = "waw"
            for r in self.readers.get(k, ()):
                if r not in deps:
                    deps[r] = "war"
        deps.pop(i, None)
        self.ops.append(dict(id=i, eng=eng, emit=emit, deps=deps, dma_key=dma_key))
        for k in reads:
            self.readers.setdefault(k, []).append(i)
        for k in writes:
            self.last_w[k] = i
            self.readers[k] = []
        return i

    def finalize(self):
        ops = self.ops
        for op in ops:
            keep = {}
            for d, kind in op["deps"].items():
                dop = ops[d]
                if dop["dma_key"] is not None:
                    keep[d] = kind
                elif dop["eng"] == op["eng"]:
                    if op["dma_key"] is not None:
                        keep[d] = kind
                    elif (kind == "raw" or STRICT_SAME_ENGINE) and op["eng"] != "pe":
                        keep[d] = kind
                else:
                    keep[d] = kind
            op["deps"] = keep
        signal = set()
        for op in ops:
            for d in op["deps"]:
                if ops[d]["dma_key"] is None:
                    signal.add(d)
        cnt = {}
        dma_cnt = {}
        for op in ops:
            if op["dma_key"] is not None:
                k = op["dma_key"]
                dma_cnt[k] = dma_cnt.get(k, 0) + 16
                op["val"] = dma_cnt[k]
                op["signal"] = True
            else:
                if op["id"] in signal:
                    cnt[op["eng"]] = cnt.get(op["eng"], 0) + 1
                    op["val"] = cnt[op["eng"]]
                    op["signal"] = True
                else:
                    op["signal"] = False
        self.dma_total = dma_cnt
        return sorted(dma_cnt.keys())


def build_program(nc):
    dt = nc.dram_tensor
    X = dt("x", [S, D], F32, kind="ExternalInput").ap()
    MEMD = dt("mem", [MEM, D], F32, kind="ExternalInput").ap()
    WIN = dt("w_in_p", [D, DIN], F32, kind="ExternalInput").ap()
    WOUT = dt("w_out_p", [D, D], F32, kind="ExternalInput").ap()
    WMEM = dt("w_mem", [D, 512], F32, kind="ExternalInput").ap()
    VECS = dt("vecs", [128, NV], F32, kind="ExternalInput").ap()
    WGBD = dt("wg_bd", [128, 8 * 128], F32, kind="ExternalInput").ap()
    CMAT = dt("cmat", [128, 4 * 128], F32, kind="ExternalInput").ap()
    MASK = dt("mask", [128, 512], F32, kind="ExternalInput").ap()
    COST = dt("cos_t", [128, S], F32, kind="ExternalInput").ap()
    SINT = dt("sin_t", [128, S], F32, kind="ExternalInput").ap()
    OUT = dt("out", [S, D], F32, kind="ExternalOutput").ap()

    R = Rec()
    es = ExitStack()
    with es:
        def sb(name, shape, dtype):
            return es.enter_context(nc.sbuf_tensor(name, shape, dtype))

        win_bf = sb("win_bf", [128, 8, DIN], BF16)
        wout_bf = sb("wout_bf", [128, 8, D], BF16)
        vec = sb("vec", [128, NV], F32)
        dv = sb("dv", [128, NDV], F32)
        cexp = sb("cexp", [128, 4], F32)
        cm_f = sb("cm_f", [128, 4, 128], F32)
        cm_bf = sb("cm_bf", [128, 4, 128], BF16)
        wg_bf = sb("wg_bf", [128, 8, 128], BF16)
        dconv = sb("dconv", [128, 16, 128], BF16)
        mask_bf = sb("mask_bf", [128, 512], BF16)
        kmT = sb("kmT", [128, 2, 256], BF16)
        vm_bf = sb("vm_bf", [128, 2, 256], BF16)
        xs = [sb(f"xs{i}", [128, D], F32) for i in range(2)]
        ot = [sb(f"ot{i}", [128, D], F32) for i in range(2)]
        xnb = [sb(f"xnb{i}", [128, D], BF16) for i in range(4)]
        xT_0 = sb("xT0", [128, 8, NT], BF16)
        xT = [xT_0, xT_0]
        ssx = sb("ssx", [128, 8], F32)
        rsx = sb("rsx", [128, 8], F32)
        cs_c = sb("cs_c", [128, NT], F32)
        cs_s = sb("cs_s", [128, NT], F32)
        u_bf0 = sb("u_bf0", [128, 4, NT + 4], BF16)
        u_bf = [u_bf0, u_bf0]
        xc_bf = sb("xc_bf", [128, 4, NT], BF16)
        tr = [sb(f"tr{i}", [128, NT], BF16) for i in range(4)]
        ti = [sb(f"ti{i}", [128, NT], BF16) for i in range(4)]
        a_t = [sb(f"a_t{i}", [128, NT], F32) for i in range(4)]
        w1 = [sb(f"w1_{i}", [128, NT], F32) for i in range(4)]
        w2 = [sb(f"w2_{i}", [128, NT], F32) for i in range(2)]
        h_t = sb("h_t", [128, 4, NT], F32)
        hcar = sb("hcar", [128, 4], F32)
        tg = [sb(f"tg{i}", [128, NT], BF16) for i in range(2)]
        sg2_0 = sb("sg2_0", [128, 8, NT], BF16)
        sg2 = [sg2_0, sg2_0]
        rstdg = sb("rstdg", [128, 3, NT], F32)
        yT = sb("yT", [128, 8, NT], BF16)
        sqq = sb("sqq", [128, 5, NT], BF16)
        t1b = [sb(f"t1b{i}", [128, NT], F32) for i in range(2)]
        qn_bf = sb("qn_bf", [128, 3, NT], BF16)
        t2b = sb("t2b", [128, NT], F32)
        t3b = sb("t3b", [128, NT], F32)
        qrot = sb("qrot", [128, 2, NT], BF16)
        kT = [sb(f"kT{i}", [128, NT], BF16) for i in range(2)]
        v_bf = [sb(f"v_bf{i}", [128, 4, 128], BF16) for i in range(2)]
        PT = [sb(f"PT{i}", [128, NT], BF16) for i in range(5)]
        rden = [sb(f"rden{i}", [128, 256], F32) for i in range(2)]
        yb_bf = sb("yb_bf", [128, 2, NT], BF16)
        qxn_bf = sb("qxn_bf", [128, 2, NT], BF16)
        rdenc = sb("rdenc", [128, NT], F32)
        yc_bf = sb("yc_bf", [128, 2, NT], BF16)
        ygt_0 = sb("ygt0", [128, NT], F32)
        ygt = [ygt_0, ygt_0]

        NPS = 8
        psb = [es.enter_context(nc.psum_tensor(f"psb{i}", [128, 512], F32)) for i in range(NPS)]
        psb_bf = [p.bitcast(BF16) for p in psb]

        ident = cm_bf[:, 0, :]
        ones_bd = cm_bf[:, 1, :]
        ones_f = cm_bf[:, 2, :]
        pmat = cm_bf[:, 3, :]

        free_banks = list(range(NPS))

        def ps_alloc():
            assert free_banks, "out of PSUM banks in record order"
            return free_banks.pop(0)

        def ps_free(b):
            free_banks.append(b)

        def ACT(out, in_, func, reads, writes, bias=None, scale=None, accum=None):
            kw = {}
            if bias is not None:
                kw["bias"] = bias
            if scale is not None:
                kw["scale"] = scale
            if accum is not None:
                kw["accum_out"] = accum
            R.add("act", lambda e: e.activation(out=out, in_=in_, func=func, **kw), reads, writes)

        def TS(eng, out, in0, s1, s2, op0, op1, reads, writes):
            if op1 is None:
                R.add(eng, lambda e: e.tensor_scalar(out=out, in0=in0, scalar1=s1, scalar2=None, op0=op0), reads, writes)
            else:
                R.add(eng, lambda e: e.tensor_scalar(out=out, in0=in0, scalar1=s1, scalar2=s2, op0=op0, op1=op1), reads, writes)

        def TT(eng, out, in0, in1, op, reads, writes):
            R.add(eng, lambda e: e.tensor_tensor(out=out, in0=in0, in1=in1, op=op), reads, writes)

        def STT(out, in0, scalar, in1, op0, op1, reads, writes):
            R.add("dve", lambda e: e.scalar_tensor_tensor(out=out, in0=in0, scalar=scalar, in1=in1, op0=op0, op1=op1), reads, writes)

        def COPY(eng, out, in_, reads, writes):
            if eng == "act":
                R.add("act", lambda e: e.copy(out=out, in_=in_), reads, writes)
            else:
                R.add(eng, lambda e: e.tensor_copy(out=out, in_=in_), reads, writes)

        def MM(out, lhsT, rhs, start, stop, reads, writes):
            R.add("pe", lambda e: e.matmul(out, lhsT=lhsT, rhs=rhs, start=start, stop=stop), reads, writes)

        def TR(out, in_, reads, writes):
            R.add("pe", lambda e: e.transpose(out, in_, ident), list(reads) + ["cm_bf"], writes)

        def DMA(out, in_, reads, writes, key):
            R.add("sp", lambda e: e.dma_start(out=out, in_=in_), reads, writes, dma_key=key)

        def POW(out, in0, expcol, n, reads, writes):
            ex = cexp[:, expcol:expcol + 1].to_broadcast([128, n])
            R.add("pool", lambda e: e.tensor_tensor(out=out, in0=in0, in1=ex, op=ALU.pow), list(reads) + ["cexp"], writes)

        def RSQ(out, in_, scale_in, reads, writes, expo=-0.5, bias_eps=True):
            kw = dict(bias=cexp[:out.shape[0] if False else 128, 2:3]) if bias_eps else {}
            R.add("act", lambda e: e.activation(out=out, in_=in_, func=AF.Ln, scale=scale_in, **kw), list(reads) + ["cexp"], writes)
            R.add("act", lambda e: e.activation(out=out, in_=out, func=AF.Exp, scale=expo), writes, writes)

        def vcol(c, n=1):
            return vec[:, c:c + n]

        def dcol(c, n=1):
            return dv[:, c:c + n]

        R.group_keys.add("c0")
        DMA(vec[:], VECS, [], ["vec"], "c0")
        DMA(cm_f[:].rearrange("p a b -> p (a b)"), CMAT, [], ["cm_f"], "c0")
        R.add("pool", lambda e: e.memset(cexp[:, 0:1], 1.0), [], ["cexp"])
        R.add("pool", lambda e: e.memset(cexp[:, 1:2], 1e-12), [], ["cexp"])
        R.add("pool", lambda e: e.memset(cexp[:, 2:3], EPS), [], ["cexp"])
        R.add("pool", lambda e: e.memset(cexp[:, 3:4], 0.0), [], ["cexp"])
        R.add("pool", lambda e: e.memset(u_bf0[:, :, 0:4], 0.0), [], [("u", 0, j) for j in range(4)])
        R.add("pool", lambda e: e.memset(hcar[:], 0.0), [], ["hcar"])
        COPY("dve", cm_bf[:], cm_f[:], ["cm_f"], ["cm_bf"])
        stage = [(ot[0][:], ["ot0"]), (ot[1][:], ["ot1"]),
                 (h_t[:, 0:2, :].rearrange("p a b -> p (a b)"), [("h", 0), ("h", 1)]),
                 (h_t[:, 2:4, :].rearrange("p a b -> p (a b)"), [("h", 2), ("h", 3)]),
                 (rstdg[:].rearrange("p a b -> p (a b)")[:, 0:768], [("rstdg", 0), ("rstdg", 1)]),
                 (rstdg[:].rearrange("p a b -> p (a b)")[:, 768:1536], [("rstdg", 1), ("rstdg", 2)]),
                 (sg2_0.bitcast(F32)[:, 0:4, :].rearrange("p a b -> p (a b)"), [("sg2", 0, g) for g in range(4)]),
                 (sg2_0.bitcast(F32)[:, 4:8, :].rearrange("p a b -> p (a b)"), [("sg2", 0, g) for g in range(4, 8)]),
                 (xc_bf.bitcast(F32)[:].rearrange("p a b -> p (a b)"), [("xc", j) for j in range(4)]),
                 (qn_bf.bitcast(F32)[:].rearrange("p a b -> p (a b)"), [("qn", i) for i in range(3)])]
        NSTG = len(stage)
        st_i = [0]
        cast_eng = ["pool", "act"]

        def staged_cast(src_ap, ncols, dst_ap, scale_ap, reads, writes):
            i = st_i[0]
            st_i[0] += 1
            buf, keys = stage[i % NSTG]
            DMA(buf[:, 0:ncols], src_ap, [], keys, "stg%d" % (i % NSTG))
            eng = cast_eng[i % len(cast_eng)]
            rd = list(keys) + list(reads)
            if scale_ap is None:
                COPY("act", dst_ap, buf[:, 0:ncols], rd, writes)
            elif eng == "act":
                ACT(dst_ap, buf[:, 0:ncols], AF.Copy, rd, writes, scale=scale_ap)
            elif eng == "pool":
                TT("pool", dst_ap, buf[:, 0:ncols], scale_ap.to_broadcast([128, ncols]), ALU.mult, rd, writes)
            else:
                TS(eng, dst_ap, buf[:, 0:ncols], scale_ap, None, ALU.mult, None, rd, writes)

        staged_cast(WGBD, 1024, wg_bf[:].rearrange("p a b -> p (a b)"), None, [], ["wg_bf"])
        staged_cast(MASK, 512, mask_bf[:], None, [], ["mask_bf"])
        TS("dve", dcol(DV_HBRG, 8), vcol(V_BRG, 8), 0.5, None, ALU.mult, None, ["vec"], ["dv_hb"])
        ACT(dcol(DV_TMP, 4), vcol(V_LAM, 4), AF.Exp, ["vec"], ["dv_tmp"], scale=-1.0)
        TS("dve", dcol(DV_TMP, 4), dcol(DV_TMP, 4), 1.0, None, ALU.add, None, ["dv_tmp"], ["dv_tmp"])
        ACT(dcol(DV_TMP, 4), dcol(DV_TMP, 4), AF.Ln, ["dv_tmp"], ["dv_tmp"])
        TS("dve", dcol(DV_CL, 4), dcol(DV_TMP, 4), -8.0, None, ALU.mult, None, ["dv_tmp"], ["dv_cl"])
        TS("dve", dcol(DV_HCL, 4), dcol(DV_TMP, 4), -4.0, None, ALU.mult, None, ["dv_tmp"], ["dv_cl"])
        ACT(dcol(DV_ESINK, 2), vcol(V_SINK, 2), AF.Exp, ["vec"], ["dv_es"])
        TS("dve", dcol(DV_OUTG, 8), vcol(V_OUTG, 8), 0.5, None, ALU.mult, None, ["vec"], ["dv_og"])
        for idx in range(16):
            TS("dve", dconv[:, idx, :], cm_f[:, 0, :], vcol(V_CONVW + idx), None, ALU.mult, None,
               ["cm_f", "vec"], [("dconv", idx)])
        wmem_bf = yT
        for k in range(8):
            staged_cast(WMEM[k * 128:(k + 1) * 128, :], 512, wmem_bf[:, k, :], vcol(V_MEMG + k), ["vec"], [("yT", k)])
        win_pieces = []
        for p3 in (1, 2, 0):
            for k in range(8):
                win_pieces.append(lambda k=k, p3=p3: staged_cast(
                        WIN[k * 128:(k + 1) * 128, p3 * 768:(p3 + 1) * 768], 768,
                        win_bf[:, k, p3 * 768:(p3 + 1) * 768], vcol(V_NORMG + k), ["vec"], [("win", k, p3)]))

        def wout_piece(k):
            sl = k % 2
            DMA(ot[sl][:], WOUT[k * 128:(k + 1) * 128, :], [], ["ot%d" % sl], "ot%d" % sl)
            ACT(wout_bf[:, k, :], ot[sl][:], AF.Copy, ["ot%d" % sl, "dv_og"], [("wout", k)], scale=dcol(DV_OUTG + k))

        memT = xT[0]
        mem_stages = []

        def mem_tile(t):
            DMA(xs[t][:], MEMD[t * 128:(t + 1) * 128, :], [], ["xs%d" % t], "xs%d" % t)
            ACT(xnb[t][:], xs[t][:], AF.Square, ["xs%d" % t], [("xnb", t), ("ssx", t)], accum=ssx[:, t:t + 1])
            RSQ(rsx[:, t:t + 1], ssx[:, t:t + 1], 1.0 / D, [("ssx", t)], [("rsx", t)])
            ACT(xnb[t][:], xs[t][:], AF.Copy, ["xs%d" % t, ("rsx", t)], [("xnb", t)], scale=rsx[:, t:t + 1])
            b = ps_alloc()
            for k in range(8):
                TR(psb_bf[b][:, k * 128:(k + 1) * 128], xnb[t][:, k * 128:(k + 1) * 128], [("xnb", t)], [("ps", b)])
            COPY("dve", xT[0][:, :, t * 128:(t + 1) * 128], psb_bf[b][:].rearrange("p (k m) -> p k m", k=8), [("ps", b)], [("xT", 0, t)])
            ps_free(b)

        mst = {}

        def mem_k1(j):
            b = ps_alloc()
            mst[j] = b
            for k in range(8):
                MM(psb[b][:, 0:256], wmem_bf[:, k, j * 128:(j + 1) * 128], memT[:, k, 0:256], k == 0, k == 7,
                   [("yT", k), ("xT", 0, 0), ("xT", 0, 1)], [("ps", b)])
            ACT(sqq[:, j, 0:256], psb[b][:, 0:256], AF.Square, [("ps", b)], [("sqq", j)])

        def mem_k2(j):
            b = mst[j]
            b2 = ps_alloc()
            MM(psb[b2][:, 0:256], ones_bd, sqq[:, j, 0:256], True, True, ["cm_bf", ("sqq", j)], [("ps", b2)])
            RSQ(t1b[j][:, 0:256], psb[b2][:, 0:256], 1.0 / 64, [("ps", b2)], [("t1b", j)])
            ps_free(b2)
            STT(kmT[:, j, :], psb[b][:, 0:256], vcol(V_XKG), t1b[j][:, 0:256], ALU.mult, ALU.mult,
                [("ps", b), "vec", ("t1b", j)], [("kmT", j)])
            ps_free(b)

        def mem_v(mb):
            b = ps_alloc()
            for k in range(8):
                MM(psb[b][:, 0:256], memT[:, k, mb * 128:(mb + 1) * 128], wmem_bf[:, k, 256:512], k == 0, k == 7,
                   [("yT", k), ("xT", 0, mb)], [("ps", b)])
            COPY("act", vm_bf[:, mb, :], psb[b][:, 0:256], [("ps", b)], [("vm", mb)])
            ps_free(b)

        mem_stages = [lambda: mem_tile(0), lambda: mem_tile(1), lambda: mem_k1(0), lambda: mem_k1(1), lambda: mem_k2(0),
                      lambda: mem_k2(1), lambda: mem_v(0), lambda: mem_v(1)]

        def norm_tile(c, t):
            T = 4 * c + t
            sl = T % 2
            col = T % 8
            DMA(xs[sl][:], X[T * 128:(T + 1) * 128, :], [], ["xs%d" % sl], "xs%d" % sl)
            R.add("dve", lambda e: e.scalar_tensor_tensor(out=xnb[t][:], in0=xs[sl][:], scalar=1.0, in1=xs[sl][:], op0=ALU.mult,
                                                          op1=ALU.mult, accum_out=ssx[:, col:col + 1]),
                  ["xs%d" % sl], [("xnb", t), ("ssx", col)])
            RSQ(rsx[:, col:col + 1], ssx[:, col:col + 1], 1.0 / D, [("ssx", col)], [("rsx", col)])
            ACT(xnb[t][:], xs[sl][:], AF.Copy, ["xs%d" % sl, ("rsx", col)], [("xnb", t)], scale=rsx[:, col:col + 1])

        def transpose_tile(c, t):
            b = ps_alloc()
            for k in range(8):
                TR(psb_bf[b][:, k * 128:(k + 1) * 128], xnb[t][:, k * 128:(k + 1) * 128], [("xnb", t)], [("ps", b)])
            COPY("act", xT[0][:, :, t * 128:(t + 1) * 128], psb_bf[b][:].rearrange("p (k m) -> p k m", k=8), [("ps", b)], [("xT", 0, t)])
            ps_free(b)

        def cs_load(c):
            t0 = c * NT
            DMA(cs_c[:], COST[:, t0:t0 + NT], [], ["cs_c"], "csc")
            DMA(cs_s[:], SINT[:, t0:t0 + NT], [], ["cs_s"], "css")

        def transposes(c):
            for t in range(4):
                transpose_tile(c, t)
            cs_load(c)

        carry = {"late": [], "oproj": []}

        def chunk(c):
            par = c % 2
            t0 = c * NT
            xTk = [("xT", 0, t) for t in range(4)]

            pending = []
            pend_rope = []

            def run_pending(n):
                for _ in range(n):
                    if pending:
                        pending.pop(0)()

            def inproj_fm(e):
                b = ps_alloc()
                for k in range(8):
                    MM(psb[b][:], win_bf[:, k, e * 128:(e + 1) * 128], xT[par][:, k, :], k == 0, k == 7,
                       [("win", k, (e * 128) // 768), ("win", k, (e * 128 + 127) // 768)] + xTk, [("ps", b)])
                return b

            def grpA_u(j):
                b = inproj_fm(j)
                if c > 0:
                    COPY("pool", u_bf[par][:, j, 0:4], u_bf[1 - par][:, j, NT:NT + 4], [("u", 0, j)], [("u", 0, j)])
                COPY("dve", u_bf[par][:, j, 4:NT + 4], psb[b][:], [("ps", b)], [("u", 0, j)])
                ps_free(b)
                pending.append(lambda: grpA_conv(j))
                if j > 0:
                    pending.append(lambda: grpA_gates(j - 1))
                if j == 3:
                    pending.append(lambda: grpA_gates(3))

            def grpA_conv(j):
                b = ps_alloc()
                for tap in range(4):
                    MM(psb[b][:], dconv[:, j * 4 + tap, :], u_bf[par][:, j, 1 + tap:1 + tap + NT], tap == 0, tap == 3,
                       [("dconv", j * 4 + tap), ("u", 0, j)], [("ps", b)])
                TS("dve", xc_bf[:, j, :], psb[b][:], vcol(V_CONVB + j), None, ALU.add, None, [("ps", b), "vec"], [("xc", j)])
                ps_free(b)

            def grpA_gates(j):
                s = j
                br = ps_alloc()
                MM(psb[br][:], wg_bf[:, j, :], xc_bf[:, j, :], True, True, ["wg_bf", ("xc", j)], [("ps", br)])
                bi = ps_alloc()
                MM(psb[bi][:], wg_bf[:, 4 + j, :], xc_bf[:, j, :], True, True, ["wg_bf", ("xc", j)], [("ps", bi)])
                ACT(tr[s][:], psb[br][:], AF.Tanh, [("ps", br), "dv_hb"], [("tr", s)], bias=dcol(DV_HBRG + j), scale=0.5)
                ps_free(br)
                ACT(ti[s][:], psb[bi][:], AF.Tanh, [("ps", bi), "dv_hb"], [("ti", s)], bias=dcol(DV_HBIG + j), scale=0.5)
                ps_free(bi)
                ACT(a_t[s][:], tr[s][:], AF.Exp, [("tr", s), "dv_cl"], [("a", s)], bias=dcol(DV_HCL + j), scale=dcol(DV_HCL + j))
                TT("pool", w1[s][:], a_t[s][:], a_t[s][:], ALU.mult, [("a", s)], [("w1", s)])

            def restR1(j):
                s = j
                ACT(w1[s][:], w1[s][:], AF.Relu, [("w1", s), "cexp"], [("w1", s)], bias=cexp[:, 0:1], scale=-1.0)

            def restR2(j):
                s = j
                ACT(w1[s][:], w1[s][:], AF.Ln, [("w1", s), "cexp"], [("w1", s)], bias=cexp[:, 1:2])
                ACT(w1[s][:], w1[s][:], AF.Exp, [("w1", s)], [("w1", s)], scale=0.5)

            def restR3(j):
                s = j
                s2 = j % 2
                STT(w2[s2][:], ti[s][:], 1.0, xc_bf[:, j, :], ALU.add, ALU.mult, [("ti", s), ("xc", j)], [("w2", s2)])
                STT(w2[s2][:], w2[s2][:], 0.5, w1[s][:], ALU.mult, ALU.mult, [("w2", s2), ("w1", s)], [("w2", s2)])
                R.add("dve", lambda e: e.tensor_tensor_scan(out=h_t[:, j, :], data0=a_t[s][:], data1=w2[s2][:],
                                                            initial=hcar[:, j:j + 1], op0=ALU.mult, op1=ALU.add),
                      [("a", s), ("w2", s2), ("hcar", j)], [("h", j)])
                COPY("pool", hcar[:, j:j + 1], h_t[:, j, NT - 1:NT], [("h", j)], [("hcar", j)])
                TT("pool", xc_bf[:, j, :], h_t[:, j, :], h_t[:, j, :], ALU.mult, [("h", j)], [("xc", j)])

            def gate(e, gi):
                b = inproj_fm(e)
                s = gi % 2
                ACT(tg[s][:], psb[b][:], AF.Tanh, [("ps", b)], [("tg", s)], scale=0.5)
                STT(sg2[par][:, gi, :], tg[s][:], 1.0, psb[b][:], ALU.add, ALU.mult, [("tg", s), ("ps", b)], [("sg2", 0, gi)])
                ps_free(b)

            def qk_in(e, idx):
                b = inproj_fm(e)
                ACT(sqq[:, idx, :], psb[b][:], AF.Square, [("ps", b)], [("sqq", idx)])
                pending.append(lambda: qk_stat(b, idx))
                pending.append(lambda: qk_norm(b, idx))
                if idx < 3:
                    pend_rope.append(lambda: qk_rope(idx))

            def qk_stat(b, idx):
                s = idx % 2
                b2 = ps_alloc()
                MM(psb[b2][:], ones_bd, sqq[:, idx, :], True, True, ["cm_bf", ("sqq", idx)], [("ps", b2)])
                RSQ(t1b[s][:], psb[b2][:], 1.0 / 64, [("ps", b2)], [("t1b", s)])
                ps_free(b2)

            def qk_norm(b, idx):
                s = idx % 2
                gcol = [V_QG, V_QG, V_KG, V_XQG, V_XQG][idx]
                if idx < 3:
                    dst, dk = qn_bf[:, idx, :], ("qn", idx)
                else:
                    dst, dk = qxn_bf[:, idx - 3, :], ("qxn", idx - 3)
                STT(dst, psb[b][:], vcol(gcol), t1b[s][:], ALU.mult, ALU.mult, [("ps", b), "vec", ("t1b", s)], [dk])
                ps_free(b)

            def qk_rope(idx):
                b = ps_alloc()
                MM(psb[b][:], pmat, qn_bf[:, idx, :], True, True, ["cm_bf", ("qn", idx)], [("ps", b)])
                TT("pool", t2b[:], qn_bf[:, idx, :], cs_c[:], ALU.mult, [("qn", idx), "cs_c"], ["t2b"])
                TT("dve", t3b[:], psb[b][:], cs_s[:], ALU.mult, [("ps", b), "cs_s"], ["t3b"])
                ps_free(b)
                if idx < 2:
                    TT("pool", qrot[:, idx, :], t2b[:], t3b[:], ALU.add, ["t2b", "t3b"], [("qrot", idx)])
                else:
                    TT("pool", kT[par][:], t2b[:], t3b[:], ALU.add, ["t2b", "t3b"], [("kT", par)])

            def v_in():
                b = ps_alloc()
                for t in range(4):
                    for k in range(8):
                        MM(psb[b][:, t * 128:(t + 1) * 128], xT[par][:, k, t * 128:(t + 1) * 128], win_bf[:, k, 11 * 128:12 * 128],
                           k == 0, k == 7, [("win", k, 1), ("xT", 0, t)], [("ps", b)])
                COPY("dve", v_bf[par][:].rearrange("p a b -> p (a b)"), psb[b][:], [("ps", b)], [("v", par)])
                ps_free(b)

            def nt(t):
                if c + 1 < NCH:
                    norm_tile(c + 1, t)

            late = carry["late"]
            carry["late"] = []

            def run_late(n):
                for _ in range(n):
                    if late:
                        late.pop(0)()

            qk_in(8, 0)
            qk_in(9, 1)
            qk_in(10, 2)
            v_in(); run_pending(1); run_late(1)
            qk_in(14, 3); run_pending(2); run_late(1)
            qk_in(15, 4); run_pending(2); run_late(1)
            for j in range(4):
                grpA_u(j)
                run_pending(2)
                run_late(1)
            run_late(len(late))
            while pending and len(pending) > 8:
                run_pending(1)
            for gi, e in enumerate([4, 5, 6, 7, 12, 13, 16, 17]):
                gate(e, gi)
                run_pending(1)
                if pend_rope:
                    pend_rope.pop(0)()
            while pending:
                run_pending(1)


            def normA():
                b = ps_alloc()
                for j in range(4):
                    MM(psb[b][:], ones_f, xc_bf[:, j, :], j == 0, j == 3, ["cm_bf", ("xc", j)], [("ps", b)])
                RSQ(rstdg[:, 0, :], psb[b][:], 1.0 / 512, [("ps", b)], [("rstdg", 0)])
                ps_free(b)

            def make_yT(kk):
                if kk < 4:
                    src, skey, gidx = h_t[:, kk, :], ("h", kk), 0
                elif kk < 6:
                    src, skey, gidx = yb_bf[:, kk - 4, :], ("yb", kk - 4), 1
                else:
                    src, skey, gidx = yc_bf[:, kk - 6, :], ("yc", kk - 6), 2
                TT("dve", ygt[0][:], src, sg2[par][:, kk, :], ALU.mult, [skey, ("sg2", 0, kk)], [("ygt", 0)])
                TT("dve", yT[:, kk, :], ygt[0][:], rstdg[:, gidx, :], ALU.mult, [("ygt", 0), ("rstdg", gidx)], [("yT", kk)])

            extras = [lambda: restR1(0), lambda: (restR1(1), nt(0)), lambda: restR2(0), lambda: (restR1(2), nt(1)), lambda: restR2(1),
                      lambda: (restR3(0), nt(2)), lambda: restR1(3), lambda: (restR2(2), nt(3)), lambda: restR3(1), lambda: restR2(3),
                      lambda: restR3(2), lambda: restR3(3), normA,
                      lambda: make_yT(0), lambda: make_yT(1), lambda: make_yT(2), lambda: make_yT(3)]

            units = [("s", n, h) for n in range(4) for h in range(2)]
            units += [("m", jx, qh, half) for jx in range(2) for qh in range(2) for half in range(2)]
            ust = {}

            def swa_sel(n, kb, hs):
                if kb == 1:
                    return (kT[par][hs, n * 128:(n + 1) * 128], ("kT", par), v_bf[par][:, n, hs], ("v", par))
                if n > 0:
                    return (kT[par][hs, (n - 1) * 128:n * 128], ("kT", par), v_bf[par][:, n - 1, hs], ("v", par))
                return (kT[1 - par][hs, 384:512], ("kT", 1 - par), v_bf[1 - par][:, 3, hs], ("v", 1 - par))

            def st1(u):
                un = units[u]
                bs = ps_alloc()
                ust[u] = bs
                if un[0] == "s":
                    _, n, h = un
                    hs = slice(h * 64, (h + 1) * 64)
                    first_blk = (c == 0 and n == 0)
                    if h == 0:
                        ust[("bp", n)] = ps_alloc()
                    kbs = [1] if first_blk else [0, 1]
                    lo = 256 if first_blk else 0
                    first_mm = True
                    for kb in kbs:
                        kap, kkey, _, _ = swa_sel(n, kb, hs)
                        for g in range(2):
                            MM(psb[bs][:, kb * 256 + g * 128: kb * 256 + (g + 1) * 128], kap, qrot[hs, g, n * 128:(n + 1) * 128],
                               first_mm, False, [kkey, ("qrot", g)], [("ps", bs)])
                            first_mm = False
                    MM(psb[bs][:, lo:512], ident, mask_bf[:, lo:512], False, True, ["cm_bf", "mask_bf"], [("ps", bs)])
                else:
                    _, jx, qh, half = un
                    hs = slice(half * 64, (half + 1) * 64)
                    if half == 0:
                        ust[("bn", jx, qh)] = ps_alloc()
                    for mb in range(2):
                        MM(psb[bs][:, mb * 256:(mb + 1) * 256], kmT[hs, jx, mb * 128:(mb + 1) * 128],
                           qxn_bf[hs, jx, qh * 256:(qh + 1) * 256], mb == 0, mb == 1, [("kmT", jx), ("qxn", jx)], [("ps", bs)])

            def st2(u):
                un = units[u]
                bs = ust[u]
                s3 = u % 5
                lo = 256 if (un[0] == "s" and c == 0 and un[1] == 0) else 0
                ACT(PT[s3][:, lo:512], psb[bs][:, lo:512], AF.Exp, [("ps", bs)], [("PT", s3)], scale=0.125)
                ps_free(bs)

            def st3(u):
                un = units[u]
                s3 = u % 5
                if un[0] == "s":
                    _, n, h = un
                    hs = slice(h * 64, (h + 1) * 64)
                    first_blk = (c == 0 and n == 0)
                    kbs = [1] if first_blk else [0, 1]
                    bp = ust[("bp", n)]
                    for g in range(2):
                        for kb in kbs:
                            _, _, vap, vkey = swa_sel(n, kb, hs)
                            rhs = PT[s3][:, kb * 256 + g * 128: kb * 256 + (g + 1) * 128]
                            MM(psb[bp][hs, g * 128:(g + 1) * 128], vap, rhs, kb == kbs[0], kb == 1, [vkey, ("PT", s3)], [("ps", bp)])
                    for g in range(2):
                        for kb in kbs:
                            rhs = PT[s3][:, kb * 256 + g * 128: kb * 256 + (g + 1) * 128]
                            MM(psb[bp][hs, 256 + g * 128: 256 + (g + 1) * 128], ones_f[:, 0:64], rhs, kb == kbs[0], kb == 1,
                               ["cm_bf", ("PT", s3)], [("ps", bp)])
                    if h == 1:
                        rs_ = n % 2
                        for g in range(2):
                            ACT(rden[rs_][:, g * 128:(g + 1) * 128], psb[bp][:, 256 + g * 128: 256 + (g + 1) * 128], AF.Ln,
                                [("ps", bp), "dv_es"], [("rden", rs_)], bias=dcol(DV_ESINK + g))
                        ACT(rden[rs_][:], rden[rs_][:], AF.Exp, [("rden", rs_)], [("rden", rs_)], scale=-1.0)
                        for g in range(2):
                            TT("dve", yb_bf[:, g, n * 128:(n + 1) * 128], psb[bp][:, g * 128:(g + 1) * 128],
                               rden[rs_][:, g * 128:(g + 1) * 128], ALU.mult, [("ps", bp), ("rden", rs_)], [("yb", g)])
                        ps_free(bp)
                else:
                    _, jx, qh, half = un
                    hh = 2 * jx + half
                    hs = slice(half * 64, (half + 1) * 64)
                    bnd = ust[("bn", jx, qh)]
                    for mb in range(2):
                        MM(psb[bnd][hs, 0:256], vm_bf[:, mb, hh * 64:(hh + 1) * 64], PT[s3][:, mb * 256:(mb + 1) * 256], mb == 0, mb == 1,
                           [("vm", mb), ("PT", s3)], [("ps", bnd)])
                    for mb in range(2):
                        MM(psb[bnd][hs, 256:512], ones_f[:, 0:64], PT[s3][:, mb * 256:(mb + 1) * 256], mb == 0, mb == 1,
                           ["cm_bf", ("PT", s3)], [("ps", bnd)])
                    if half == 1:
                        qs = slice(qh * 256, (qh + 1) * 256)
                        ACT(rdenc[:, qs], psb[bnd][:, 256:512], AF.Ln, [("ps", bnd)], [("rdenc", qh)])
                        ACT(rdenc[:, qs], rdenc[:, qs], AF.Exp, [("rdenc", qh)], [("rdenc", qh)], scale=-1.0)
                        TT("dve", yc_bf[:, jx, qs], psb[bnd][:, 0:256], rdenc[:, qs], ALU.mult, [("ps", bnd), ("rdenc", qh)], [("yc", jx)])
                        ps_free(bnd)

            NU = len(units)
            oproj_prev = carry["oproj"]
            carry["oproj"] = []
            for i in range(NU + 5):
                if i < NU:
                    st1(i)
                if 0 <= i - 1 < NU:
                    st2(i - 1)
                if 0 <= i - 5 < NU:
                    st3(i - 5)
                if extras:
                    extras.pop(0)()
                if len(oproj_prev) > 2 and i >= 2:
                    oproj_prev.pop(0)()
                if i in (3, 6, 9, 12) and c + 1 < NCH:
                    transpose_tile(c + 1, (i - 3) // 3)
                if i == 11:
                    if c + 1 < NCH:
                        cs_load(c + 1)
                    while oproj_prev:
                        oproj_prev.pop(0)()
            assert not oproj_prev
            while extras:
                extras.pop(0)()

            def normBC(gidx, ysrc, ykey):
                for jj in range(2):
                    TT("pool", PT[jj][:], ysrc[:, jj, :], ysrc[:, jj, :], ALU.mult, [(ykey, jj)], [("PT", jj)])
                b = ps_alloc()
                for jj in range(2):
                    MM(psb[b][:], ones_f, PT[jj][:], jj == 0, jj == 1, ["cm_bf", ("PT", jj)], [("ps", b)])
                RSQ(rstdg[:, gidx, :], psb[b][:], 1.0 / 256, [("ps", b)], [("rstdg", gidx)])
                ps_free(b)

            def oproj_grp(t, dh):
                T = 4 * c + t
                sl = T % 2
                okey = "ot%d" % sl
                if dh == 0:
                    DMA(ot[sl][:], X[T * 128:(T + 1) * 128, :], [], [okey], okey)
                b = ps_alloc()
                for kk in range(8):
                    MM(psb[b][:], yT[:, kk, t * 128:(t + 1) * 128], wout_bf[:, kk, dh * 512:(dh + 1) * 512], kk == 0, kk == 7,
                       [("yT", kk), ("wout", kk)], [("ps", b)])
                TT("dve", ot[sl][:, dh * 512:(dh + 1) * 512], psb[b][:], ot[sl][:, dh * 512:(dh + 1) * 512], ALU.add,
                   [("ps", b), okey], [okey])
                ps_free(b)
                if dh == 1:
                    DMA(OUT[T * 128:(T + 1) * 128, :], ot[sl][:], [okey], [okey], okey)

            carry["late"] = [lambda: normBC(1, yb_bf, "yb"), lambda: normBC(2, yc_bf, "yc"),
                             lambda: make_yT(4), lambda: make_yT(5), lambda: make_yT(6), lambda: make_yT(7)]
            carry["oproj"] = [(lambda t=t, dh=dh: oproj_grp(t, dh)) for t in range(4) for dh in range(2)]

        side = mem_stages + [lambda: norm_tile(0, 0), lambda: norm_tile(0, 1), lambda: norm_tile(0, 2), lambda: norm_tile(0, 3)]
        side.append(lambda: transposes(0))
        for i, piece in enumerate(win_pieces):
            piece()
            if side:
                side.pop(0)()
        while side:
            side.pop(0)()
        carry["late"] = [(lambda k=k: wout_piece(k)) for k in range(8)]
        for c in range(NCH):
            chunk(c)
        for fn in carry["late"]:
            fn()
        for fn in carry["oproj"]:
            fn()

        dma_keys = R.finalize()
        sems = {}
        for e in ("pe", "act", "dve", "pool"):
            sems[e] = es.enter_context(nc.semaphore("s_" + e))
        for k in dma_keys:
            sems[("dma", k)] = es.enter_context(nc.semaphore("d_" + k))
        ops = R.ops

        def emit_engine(ename, eng):
            waited = {}
            for op in ops:
                if op["eng"] != ename:
                    continue
                for d in sorted(op["deps"]):
                    dop = ops[d]
                    if dop["dma_key"] is not None:
                        key = ("dma", dop["dma_key"])
                        val = R.dma_total[dop["dma_key"]] if dop["dma_key"] in R.group_keys else dop["val"]
                    else:
                        key = dop["eng"]
                        val = dop["val"]
                    if waited.get(key, 0) < val:
                        eng.wait_ge(sems[key], val)
                        waited[key] = val
                ins = op["emit"](eng)
                if op["dma_key"] is not None:
                    ins.then_inc(sems[("dma", op["dma_key"])], 16)
                elif op["signal"]:
                    ins.then_inc(sems[ename], 1)
            if ename == "sp":
                for k in ("ot0", "ot1"):
                    eng.wait_ge(sems[("dma", k)], R.dma_total[k])

        block = es.enter_context(nc.Block())

        @block.tensor
        def _(e):
            emit_engine("pe", e)

        @block.scalar
        def _(e):
            emit_engine("act", e)

        @block.vector
        def _(e):
            emit_engine("dve", e)

        @block.gpsimd
        def _(e):
            emit_engine("pool", e)

        @block.sync
        def _(e):
            emit_engine("sp", e)
    return nc


def _host_consts():
    ident = np.eye(128, dtype=np.float32)
    ones_bd = np.zeros((128, 128), np.float32)
    ones_bd[0:64, 0:64] = 1.0
    ones_bd[64:128, 64:128] = 1.0
    ones_f = np.ones((128, 128), np.float32)
    pm = np.zeros((128, 128), np.float32)
    for base in (0, 64):
        for i in range(8):
            pm[base + i + 8, base + i] = -1.0
            pm[base + i, base + i + 8] = 1.0
    cmat = np.concatenate([ident, ones_bd, ones_f, pm], axis=1)
    kk = np.arange(128)[:, None]
    qq = np.arange(128)[None, :]
    prev = np.where(kk > qq, 0.0, -30000.0).astype(np.float32)
    cur = np.where(kk <= qq, 0.0, -30000.0).astype(np.float32)
    mask = np.concatenate([prev, prev, cur, cur], axis=1)
    pos = np.arange(S, dtype=np.float32)
    inv_freq = (np.float32(500000.0) ** (-(np.arange(0, 16, 2, dtype=np.float32) / np.float32(16.0)))).astype(np.float32)
    ang = pos[None, :] * inv_freq[:, None]
    cos_t = np.ones((128, S), np.float32)
    sin_t = np.zeros((128, S), np.float32)
    for base in (0, 64):
        for i in range(16):
            cos_t[base + i] = np.cos(ang[i % 8])
            sin_t[base + i] = np.sin(ang[i % 8])
    return cmat, mask, cos_t, sin_t


def kernel(x, mem, norm_g, mem_norm_g, w_in, conv_w, conv_b, w_rg, b_rg, w_ig, b_ig, lru_lambda,
           q_norm_g, k_norm_g, sinks, w_mem_kv, xq_norm_g, xk_norm_g, out_norm_g, w_out):
    f = lambda a: np.ascontiguousarray(np.asarray(a, dtype=np.float32))
    x, mem = f(x), f(mem)
    w_in0, w_out0, w_mem0 = f(w_in)[0], f(w_out)[0], f(w_mem_kv)[0]
    cols = []
    cols += list(range(0, 1024))
    sq0 = 1024
    cols += list(range(sq0 + 0, sq0 + 64)) + list(range(sq0 + 128, sq0 + 192))
    cols += list(range(sq0 + 64, sq0 + 128)) + list(range(sq0 + 192, sq0 + 256))
    cols += list(range(1280, 1408))
    cols += list(range(1408, 1536))
    g0 = 1536
    cols += list(range(g0 + 0, g0 + 64)) + list(range(g0 + 128, g0 + 192))
    cols += list(range(g0 + 64, g0 + 128)) + list(range(g0 + 192, g0 + 256))
    cols += list(range(1792, 2304))
    cols = np.array(cols)
    w_in_p = np.ascontiguousarray(w_in0[:, cols])
    rows = list(range(0, 512))
    rows += list(range(512, 576)) + list(range(640, 704))
    rows += list(range(576, 640)) + list(range(704, 768))
    rows += list(range(768, 1024))
    rows = np.array(rows)
    w_out_p = np.ascontiguousarray(w_out0[rows, :])
    vecs = np.zeros((128, NV), np.float32)
    vecs[:, V_NORMG:V_NORMG + 8] = f(norm_g)[0].reshape(8, 128).T
    vecs[:, V_MEMG:V_MEMG + 8] = f(mem_norm_g)[0].reshape(8, 128).T
    vecs[:, V_OUTG:V_OUTG + 8] = f(out_norm_g)[0][rows].reshape(8, 128).T
    cw = f(conv_w)[0]
    for j in range(4):
        for tap in range(4):
            vecs[:, V_CONVW + j * 4 + tap] = cw[tap, j * 128:(j + 1) * 128]
    vecs[:, V_CONVB:V_CONVB + 4] = f(conv_b)[0].reshape(4, 128).T
    vecs[:, V_BRG:V_BRG + 4] = f(b_rg)[0].reshape(4, 128).T
    vecs[:, V_BIG:V_BIG + 4] = f(b_ig)[0].reshape(4, 128).T
    vecs[:, V_LAM:V_LAM + 4] = f(lru_lambda)[0].reshape(4, 128).T
    vecs[:, V_QG] = np.tile(f(q_norm_g)[0], 2)
    vecs[:, V_KG] = np.tile(f(k_norm_g)[0], 2)
    vecs[:, V_XQG] = np.tile(f(xq_norm_g)[0], 2)
    vecs[:, V_XKG] = np.tile(f(xk_norm_g)[0], 2)
    sk = f(sinks)[0]
    for g in range(2):
        vecs[0:64, V_SINK + g] = sk[g]
        vecs[64:128, V_SINK + g] = sk[2 + g]
    wg = np.zeros((128, 8, 128), np.float32)
    wr, wi = f(w_rg)[0], f(w_ig)[0]
    for j in range(4):
        for l in range(2):
            wg[l * 64:(l + 1) * 64, j, l * 64:(l + 1) * 64] = wr[2 * j + l]
            wg[l * 64:(l + 1) * 64, 4 + j, l * 64:(l + 1) * 64] = wi[2 * j + l]
    wg = np.ascontiguousarray(wg.reshape(128, 1024))
    cmat, mask, cos_t, sin_t = _host_consts()

    nc = bass.Bass("TRN2", target_bir_lowering=False)
    build_program(nc)
    in_maps = []
    for b in range(8):
        in_maps.append({
            "x": x[b], "mem": mem[b], "w_in_p": w_in_p, "w_out_p": w_out_p, "w_mem": w_mem0,
            "vecs": vecs, "wg_bd": wg, "cmat": cmat, "mask": mask, "cos_t": cos_t, "sin_t": sin_t,
        })
    res = run_bass_kernel_spmd(nc, in_maps, core_ids=list(range(8)))
    out = np.stack([np.asarray(r["out"], dtype=np.float32) for r in res.results], axis=0)
    return out
```

```python
import math
from contextlib import ExitStack

import numpy as np
import concourse.bass as bass
import concourse.mybir as mybir
from concourse.bass_utils import run_bass_kernel_spmd

F32 = mybir.dt.float32
BF16 = mybir.dt.bfloat16
ALU = mybir.AluOpType
AF = mybir.ActivationFunctionType

S = 4096
D = 1024
DIN = 2304
NT = 512
NCH = S // NT
EPS = 1e-6
MEM = 256
STRICT_SAME_ENGINE = True

V_NORMG, V_MEMG, V_OUTG, V_CONVW, V_CONVB, V_BRG, V_BIG, V_LAM = 0, 8, 16, 24, 40, 44, 48, 52
V_QG, V_KG, V_XQG, V_XKG, V_SINK = 56, 57, 58, 59, 60
NV = 64
DV_HBRG, DV_HBIG, DV_CL, DV_HCL, DV_ESINK, DV_OUTG, DV_TMP = 0, 4, 8, 12, 16, 18, 26
NDV = 32


class Rec:
    def __init__(self):
        self.ops = []
        self.last_w = {}
        self.readers = {}
        self.group_keys = set()

    def add(self, eng, emit, reads=(), writes=(), dma_key=None):
        i = len(self.ops)
        deps = {}
        for k in reads:
            w = self.last_w.get(k)
            if w is not None:
                deps[w] = "raw"
        for k in writes:
            w = self.last_w.get(k)
            if w is not None and w not in deps:
                deps[w] = "waw"
            for r in self.readers.get(k, ()):
                if r not in deps:
                    deps[r] = "war"
        deps.pop(i, None)
        self.ops.append(dict(id=i, eng=eng, emit=emit, deps=deps, dma_key=dma_key))
        for k in reads:
            self.readers.setdefault(k, []).append(i)
        for k in writes:
            self.last_w[k] = i
            self.readers[k] = []
        return i

    def finalize(self):
        ops = self.ops
        for op in ops:
            keep = {}
            for d, kind in op["deps"].items():
                dop = ops[d]
                if dop["dma_key"] is not None:
                    keep[d] = kind
                elif dop["eng"] == op["eng"]:
                    if op["dma_key"] is not None:
                        keep[d] = kind
                    elif (kind == "raw" or STRICT_SAME_ENGINE) and op["eng"] != "pe":
                        keep[d] = kind
                else:
                    keep[d] = kind
            op["deps"] = keep
        signal = set()
        for op in ops:
            for d in op["deps"]:
                if ops[d]["dma_key"] is None:
                    signal.add(d)
        cnt = {}
        dma_cnt = {}
        for op in ops:
            if op["dma_key"] is not None:
                k = op["dma_key"]
                dma_cnt[k] = dma_cnt.get(k, 0) + 16
                op["val"] = dma_cnt[k]
                op["signal"] = True
            else:
                if op["id"] in signal:
                    cnt[op["eng"]] = cnt.get(op["eng"], 0) + 1
                    op["val"] = cnt[op["eng"]]
                    op["signal"] = True
                else:
                    op["signal"] = False
        self.dma_total = dma_cnt
        return sorted(dma_cnt.keys())


def build_program(nc):
    dt = nc.dram_tensor
    X = dt("x", [S, D], F32, kind="ExternalInput").ap()
    MEMD = dt("mem", [MEM, D], F32, kind="ExternalInput").ap()
    WIN = dt("w_in_p", [D, DIN], F32, kind="ExternalInput").ap()
    WOUT = dt("w_out_p", [D, D], F32, kind="ExternalInput").ap()
    WMEM = dt("w_mem", [D, 512], F32, kind="ExternalInput").ap()
    VECS = dt("vecs", [128, NV], F32, kind="ExternalInput").ap()
    WGBD = dt("wg_bd", [128, 8 * 128], F32, kind="ExternalInput").ap()
    CMAT = dt("cmat", [128, 4 * 128], F32, kind="ExternalInput").ap()
    MASK = dt("mask", [128, 512], F32, kind="ExternalInput").ap()
    COST = dt("cos_t", [128, S], F32, kind="ExternalInput").ap()
    SINT = dt("sin_t", [128, S], F32, kind="ExternalInput").ap()
    OUT = dt("out", [S, D], F32, kind="ExternalOutput").ap()

    R = Rec()
    es = ExitStack()
    with es:
        def sb(name, shape, dtype):
            return es.enter_context(nc.sbuf_tensor(name, shape, dtype))

        win_bf = sb("win_bf", [128, 8, DIN], BF16)
        wout_bf = sb("wout_bf", [128, 8, D], BF16)
        vec = sb("vec", [128, NV], F32)
        dv = sb("dv", [128, NDV], F32)
        cexp = sb("cexp", [128, 4], F32)
        cm_f = sb("cm_f", [128, 4, 128], F32)
        cm_bf = sb("cm_bf", [128, 4, 128], BF16)
        wg_bf = sb("wg_bf", [128, 8, 128], BF16)
        dconv = sb("dconv", [128, 16, 128], BF16)
        mask_bf = sb("mask_bf", [128, 512], BF16)
        kmT = sb("kmT", [128, 2, 256], BF16)
        vm_bf = sb("vm_bf", [128, 2, 256], BF16)
        xs = [sb(f"xs{i}", [128, D], F32) for i in range(2)]
        ot = [sb(f"ot{i}", [128, D], F32) for i in range(2)]
        xnb = [sb(f"xnb{i}", [128, D], BF16) for i in range(4)]
        xT_0 = sb("xT0", [128, 8, NT], BF16)
        xT = [xT_0, xT_0]
        ssx = sb("ssx", [128, 8], F32)
        rsx = sb("rsx", [128, 8], F32)
        cs_c = sb("cs_c", [128, NT], F32)
        cs_s = sb("cs_s", [128, NT], F32)
        u_bf0 = sb("u_bf0", [128, 4, NT + 4], BF16)
        u_bf = [u_bf0, u_bf0]
        xc_bf = sb("xc_bf", [128, 4, NT], BF16)
        tr = [sb(f"tr{i}", [128, NT], BF16) for i in range(4)]
        ti = [sb(f"ti{i}", [128, NT], BF16) for i in range(4)]
        a_t = [sb(f"a_t{i}", [128, NT], F32) for i in range(4)]
        w1 = [sb(f"w1_{i}", [128, NT], F32) for i in range(4)]
        w2 = [sb(f"w2_{i}", [128, NT], F32) for i in range(2)]
        h_t = sb("h_t", [128, 4, NT], F32)
        hcar = sb("hcar", [128, 4], F32)
        tg = [sb(f"tg{i}", [128, NT], BF16) for i in range(2)]
        sg2_0 = sb("sg2_0", [128, 8, NT], BF16)
        sg2 = [sg2_0, sg2_0]
        rstdg = sb("rstdg", [128, 3, NT], F32)
        yT = sb("yT", [128, 8, NT], BF16)
        sqq = sb("sqq", [128, 5, NT], BF16)
        t1b = [sb(f"t1b{i}", [128, NT], F32) for i in range(2)]
        qn_bf = sb("qn_bf", [128, 3, NT], BF16)
        t2b = sb("t2b", [128, NT], F32)
        t3b = sb("t3b", [128, NT], F32)
        qrot = sb("qrot", [128, 2, NT], BF16)
        kT = [sb(f"kT{i}", [128, NT], BF16) for i in range(2)]
        v_bf = [sb(f"v_bf{i}", [128, 4, 128], BF16) for i in range(2)]
        PT = [sb(f"PT{i}", [128, NT], BF16) for i in range(5)]
        rden = [sb(f"rden{i}", [128, 256], F32) for i in range(2)]
        yb_bf = sb("yb_bf", [128, 2, NT], BF16)
        qxn_bf = sb("qxn_bf", [128, 2, NT], BF16)
        rdenc = sb("rdenc", [128, NT], F32)
        yc_bf = sb("yc_bf", [128, 2, NT], BF16)
        ygt_0 = sb("ygt0", [128, NT], F32)
        ygt = [ygt_0, ygt_0]

        NPS = 8
        psb = [es.enter_context(nc.psum_tensor(f"psb{i}", [128, 512], F32)) for i in range(NPS)]
        psb_bf = [p.bitcast(BF16) for p in psb]

        ident = cm_bf[:, 0, :]
        ones_bd = cm_bf[:, 1, :]
        ones_f = cm_bf[:, 2, :]
        pmat = cm_bf[:, 3, :]

        free_banks = list(range(NPS))

        def ps_alloc():
            assert free_banks, "out of PSUM banks in record order"
            return free_banks.pop(0)

        def ps_free(b):
            free_banks.append(b)

        def ACT(out, in_, func, reads, writes, bias=None, scale=None, accum=None):
            kw = {}
            if bias is not None:
                kw["bias"] = bias
            if scale is not None:
                kw["scale"] = scale
            if accum is not None:
                kw["accum_out"] = accum
            R.add("act", lambda e: e.activation(out=out, in_=in_, func=func, **kw), reads, writes)

        def TS(eng, out, in0, s1, s2, op0, op1, reads, writes):
            if op1 is None:
                R.add(eng, lambda e: e.tensor_scalar(out=out, in0=in0, scalar1=s1, scalar2=None, op0=op0), reads, writes)
            else:
                R.add(eng, lambda e: e.tensor_scalar(out=out, in0=in0, scalar1=s1, scalar2=s2, op0=op0, op1=op1), reads, writes)

        def TT(eng, out, in0, in1, op, reads, writes):
            R.add(eng, lambda e: e.tensor_tensor(out=out, in0=in0, in1=in1, op=op), reads, writes)

        def STT(out, in0, scalar, in1, op0, op1, reads, writes):
            R.add("dve", lambda e: e.scalar_tensor_tensor(out=out, in0=in0, scalar=scalar, in1=in1, op0=op0, op1=op1), reads, writes)

        def COPY(eng, out, in_, reads, writes):
            if eng == "act":
                R.add("act", lambda e: e.copy(out=out, in_=in_), reads, writes)
            else:
                R.add(eng, lambda e: e.tensor_copy(out=out, in_=in_), reads, writes)

        def MM(out, lhsT, rhs, start, stop, reads, writes):
            R.add("pe", lambda e: e.matmul(out, lhsT=lhsT, rhs=rhs, start=start, stop=stop), reads, writes)

        def TR(out, in_, reads, writes):
            R.add("pe", lambda e: e.transpose(out, in_, ident), list(reads) + ["cm_bf"], writes)

        def DMA(out, in_, reads, writes, key):
            R.add("sp", lambda e: e.dma_start(out=out, in_=in_), reads, writes, dma_key=key)

        def POW(out, in0, expcol, n, reads, writes):
            ex = cexp[:, expcol:expcol + 1].to_broadcast([128, n])
            R.add("pool", lambda e: e.tensor_tensor(out=out, in0=in0, in1=ex, op=ALU.pow), list(reads) + ["cexp"], writes)

        def RSQ(out, in_, scale_in, reads, writes, expo=-0.5, bias_eps=True):
            kw = dict(bias=cexp[:out.shape[0] if False else 128, 2:3]) if bias_eps else {}
            R.add("act", lambda e: e.activation(out=out, in_=in_, func=AF.Ln, scale=scale_in, **kw), list(reads) + ["cexp"], writes)
            R.add("act", lambda e: e.activation(out=out, in_=out, func=AF.Exp, scale=expo), writes, writes)

        def vcol(c, n=1):
            return vec[:, c:c + n]

        def dcol(c, n=1):
            return dv[:, c:c + n]

        R.group_keys.add("c0")
        DMA(vec[:], VECS, [], ["vec"], "c0")
        DMA(cm_f[:].rearrange("p a b -> p (a b)"), CMAT, [], ["cm_f"], "c0")
        R.add("pool", lambda e: e.memset(cexp[:, 0:1], 1.0), [], ["cexp"])
        R.add("pool", lambda e: e.memset(cexp[:, 1:2], 1e-12), [], ["cexp"])
        R.add("pool", lambda e: e.memset(cexp[:, 2:3], EPS), [], ["cexp"])
        R.add("pool", lambda e: e.memset(cexp[:, 3:4], 0.0), [], ["cexp"])
        R.add("pool", lambda e: e.memset(u_bf0[:, :, 0:4], 0.0), [], [("u", 0, j) for j in range(4)])
        R.add("pool", lambda e: e.memset(hcar[:], 0.0), [], ["hcar"])
        COPY("dve", cm_bf[:], cm_f[:], ["cm_f"], ["cm_bf"])
        stage = [(ot[0][:], ["ot0"]), (ot[1][:], ["ot1"]),
                 (h_t[:, 0:2, :].rearrange("p a b -> p (a b)"), [("h", 0), ("h", 1)]),
                 (h_t[:, 2:4, :].rearrange("p a b -> p (a b)"), [("h", 2), ("h", 3)]),
                 (rstdg[:].rearrange("p a b -> p (a b)")[:, 0:768], [("rstdg", 0), ("rstdg", 1)]),
                 (rstdg[:].rearrange("p a b -> p (a b)")[:, 768:1536], [("rstdg", 1), ("rstdg", 2)]),
                 (sg2_0.bitcast(F32)[:, 0:4, :].rearrange("p a b -> p (a b)"), [("sg2", 0, g) for g in range(4)]),
                 (sg2_0.bitcast(F32)[:, 4:8, :].rearrange("p a b -> p (a b)"), [("sg2", 0, g) for g in range(4, 8)]),
                 (xc_bf.bitcast(F32)[:].rearrange("p a b -> p (a b)"), [("xc", j) for j in range(4)]),
                 (qn_bf.bitcast(F32)[:].rearrange("p a b -> p (a b)"), [("qn", i) for i in range(3)])]
        NSTG = len(stage)
        st_i = [0]
        cast_eng = ["pool", "act"]

        def staged_cast(src_ap, ncols, dst_ap, scale_ap, reads, writes):
            i = st_i[0]
            st_i[0] += 1
            buf, keys = stage[i % NSTG]
            DMA(buf[:, 0:ncols], src_ap, [], keys, "stg%d" % (i % NSTG))
            eng = cast_eng[i % len(cast_eng)]
            rd = list(keys) + list(reads)
            if scale_ap is None:
                COPY("act", dst_ap, buf[:, 0:ncols], rd, writes)
            elif eng == "act":
                ACT(dst_ap, buf[:, 0:ncols], AF.Copy, rd, writes, scale=scale_ap)
            elif eng == "pool":
                TT("pool", dst_ap, buf[:, 0:ncols], scale_ap.to_broadcast([128, ncols]), ALU.mult, rd, writes)
            else:
                TS(eng, dst_ap, buf[:, 0:ncols], scale_ap, None, ALU.mult, None, rd, writes)

        staged_cast(WGBD, 1024, wg_bf[:].rearrange("p a b -> p (a b)"), None, [], ["wg_bf"])
        staged_cast(MASK, 512, mask_bf[:], None, [], ["mask_bf"])
        TS("dve", dcol(DV_HBRG, 8), vcol(V_BRG, 8), 0.5, None, ALU.mult, None, ["vec"], ["dv_hb"])
        ACT(dcol(DV_TMP, 4), vcol(V_LAM, 4), AF.Exp, ["vec"], ["dv_tmp"], scale=-1.0)
        TS("dve", dcol(DV_TMP, 4), dcol(DV_TMP, 4), 1.0, None, ALU.add, None, ["dv_tmp"], ["dv_tmp"])
        ACT(dcol(DV_TMP, 4), dcol(DV_TMP, 4), AF.Ln, ["dv_tmp"], ["dv_tmp"])
        TS("dve", dcol(DV_CL, 4), dcol(DV_TMP, 4), -8.0, None, ALU.mult, None, ["dv_tmp"], ["dv_cl"])
        TS("dve", dcol(DV_HCL, 4), dcol(DV_TMP, 4), -4.0, None, ALU.mult, None, ["dv_tmp"], ["dv_cl"])
        ACT(dcol(DV_ESINK, 2), vcol(V_SINK, 2), AF.Exp, ["vec"], ["dv_es"])
        TS("dve", dcol(DV_OUTG, 8), vcol(V_OUTG, 8), 0.5, None, ALU.mult, None, ["vec"], ["dv_og"])
        for idx in range(16):
            TS("dve", dconv[:, idx, :], cm_f[:, 0, :], vcol(V_CONVW + idx), None, ALU.mult, None,
               ["cm_f", "vec"], [("dconv", idx)])
        wmem_bf = yT
        for k in range(8):
            staged_cast(WMEM[k * 128:(k + 1) * 128, :], 512, wmem_bf[:, k, :], vcol(V_MEMG + k), ["vec"], [("yT", k)])
        win_pieces = []
        for p3 in (1, 2, 0):
            for k in range(8):
                win_pieces.append(lambda k=k, p3=p3: staged_cast(
                        WIN[k * 128:(k + 1) * 128, p3 * 768:(p3 + 1) * 768], 768,
                        win_bf[:, k, p3 * 768:(p3 + 1) * 768], vcol(V_NORMG + k), ["vec"], [("win", k, p3)]))

        def wout_piece(k):
            sl = k % 2
            DMA(ot[sl][:], WOUT[k * 128:(k + 1) * 128, :], [], ["ot%d" % sl], "ot%d" % sl)
            ACT(wout_bf[:, k, :], ot[sl][:], AF.Copy, ["ot%d" % sl, "dv_og"], [("wout", k)], scale=dcol(DV_OUTG + k))

        memT = xT[0]
        mem_stages = []

        def mem_tile(t):
            DMA(xs[t][:], MEMD[t * 128:(t + 1) * 128, :], [], ["xs%d" % t], "xs%d" % t)
            ACT(xnb[t][:], xs[t][:], AF.Square, ["xs%d" % t], [("xnb", t), ("ssx", t)], accum=ssx[:, t:t + 1])
            RSQ(rsx[:, t:t + 1], ssx[:, t:t + 1], 1.0 / D, [("ssx", t)], [("rsx", t)])
            ACT(xnb[t][:], xs[t][:], AF.Copy, ["xs%d" % t, ("rsx", t)], [("xnb", t)], scale=rsx[:, t:t + 1])
            b = ps_alloc()
            for k in range(8):
                TR(psb_bf[b][:, k * 128:(k + 1) * 128], xnb[t][:, k * 128:(k + 1) * 128], [("xnb", t)], [("ps", b)])
            COPY("dve", xT[0][:, :, t * 128:(t + 1) * 128], psb_bf[b][:].rearrange("p (k m) -> p k m", k=8), [("ps", b)], [("xT", 0, t)])
            ps_free(b)

        mst = {}

        def mem_k1(j):
            b = ps_alloc()
            mst[j] = b
            for k in range(8):
                MM(psb[b][:, 0:256], wmem_bf[:, k, j * 128:(j + 1) * 128], memT[:, k, 0:256], k == 0, k == 7,
                   [("yT", k), ("xT", 0, 0), ("xT", 0, 1)], [("ps", b)])
            ACT(sqq[:, j, 0:256], psb[b][:, 0:256], AF.Square, [("ps", b)], [("sqq", j)])

        def mem_k2(j):
            b = mst[j]
            b2 = ps_alloc()
            MM(psb[b2][:, 0:256], ones_bd, sqq[:, j, 0:256], True, True, ["cm_bf", ("sqq", j)], [("ps", b2)])
            RSQ(t1b[j][:, 0:256], psb[b2][:, 0:256], 1.0 / 64, [("ps", b2)], [("t1b", j)])
            ps_free(b2)
            STT(kmT[:, j, :], psb[b][:, 0:256], vcol(V_XKG), t1b[j][:, 0:256], ALU.mult, ALU.mult,
                [("ps", b), "vec", ("t1b", j)], [("kmT", j)])
            ps_free(b)

        def mem_v(mb):
            b = ps_alloc()
            for k in range(8):
                MM(psb[b][:, 0:256], memT[:, k, mb * 128:(mb + 1) * 128], wmem_bf[:, k, 256:512], k == 0, k == 7,
                   [("yT", k), ("xT", 0, mb)], [("ps", b)])
            COPY("act", vm_bf[:, mb, :], psb[b][:, 0:256], [("ps", b)], [("vm", mb)])
            ps_free(b)

        mem_stages = [lambda: mem_tile(0), lambda: mem_tile(1), lambda: mem_k1(0), lambda: mem_k1(1), lambda: mem_k2(0),
                      lambda: mem_k2(1), lambda: mem_v(0), lambda: mem_v(1)]

        def norm_tile(c, t):
            T = 4 * c + t
            sl = T % 2
            col = T % 8
            DMA(xs[sl][:], X[T * 128:(T + 1) * 128, :], [], ["xs%d" % sl], "xs%d" % sl)
            R.add("dve", lambda e: e.scalar_tensor_tensor(out=xnb[t][:], in0=xs[sl][:], scalar=1.0, in1=xs[sl][:], op0=ALU.mult,
                                                          op1=ALU.mult, accum_out=ssx[:, col:col + 1]),
                  ["xs%d" % sl], [("xnb", t), ("ssx", col)])
            RSQ(rsx[:, col:col + 1], ssx[:, col:col + 1], 1.0 / D, [("ssx", col)], [("rsx", col)])
            ACT(xnb[t][:], xs[sl][:], AF.Copy, ["xs%d" % sl, ("rsx", col)], [("xnb", t)], scale=rsx[:, col:col + 1])

        def transpose_tile(c, t):
            b = ps_alloc()
            for k in range(8):
                TR(psb_bf[b][:, k * 128:(k + 1) * 128], xnb[t][:, k * 128:(k + 1) * 128], [("xnb", t)], [("ps", b)])
            COPY("act", xT[0][:, :, t * 128:(t + 1) * 128], psb_bf[b][:].rearrange("p (k m) -> p k m", k=8), [("ps", b)], [("xT", 0, t)])
            ps_free(b)

        def cs_load(c):
            t0 = c * NT
            DMA(cs_c[:], COST[:, t0:t0 + NT], [], ["cs_c"], "csc")
            DMA(cs_s[:], SINT[:, t0:t0 + NT], [], ["cs_s"], "css")

        def transposes(c):
            for t in range(4):
                transpose_tile(c, t)
            cs_load(c)

        carry = {"late": [], "oproj": []}

        def chunk(c):
            par = c % 2
            t0 = c * NT
            xTk = [("xT", 0, t) for t in range(4)]

            pending = []
            pend_rope = []

            def run_pending(n):
                for _ in range(n):
                    if pending:
                        pending.pop(0)()

            def inproj_fm(e):
                b = ps_alloc()
                for k in range(8):
                    MM(psb[b][:], win_bf[:, k, e * 128:(e + 1) * 128], xT[par][:, k, :], k == 0, k == 7,
                       [("win", k, (e * 128) // 768), ("win", k, (e * 128 + 127) // 768)] + xTk, [("ps", b)])
                return b

            def grpA_u(j):
                b = inproj_fm(j)
                if c > 0:
                    COPY("pool", u_bf[par][:, j, 0:4], u_bf[1 - par][:, j, NT:NT + 4], [("u", 0, j)], [("u", 0, j)])
                COPY("dve", u_bf[par][:, j, 4:NT + 4], psb[b][:], [("ps", b)], [("u", 0, j)])
                ps_free(b)
                pending.append(lambda: grpA_conv(j))
                if j > 0:
                    pending.append(lambda: grpA_gates(j - 1))
                if j == 3:
                    pending.append(lambda: grpA_gates(3))

            def grpA_conv(j):
                b = ps_alloc()
                for tap in range(4):
                    MM(psb[b][:], dconv[:, j * 4 + tap, :], u_bf[par][:, j, 1 + tap:1 + tap + NT], tap == 0, tap == 3,
                       [("dconv", j * 4 + tap), ("u", 0, j)], [("ps", b)])
                TS("dve", xc_bf[:, j, :], psb[b][:], vcol(V_CONVB + j), None, ALU.add, None, [("ps", b), "vec"], [("xc", j)])
                ps_free(b)

            def grpA_gates(j):
                s = j
                br = ps_alloc()
                MM(psb[br][:], wg_bf[:, j, :], xc_bf[:, j, :], True, True, ["wg_bf", ("xc", j)], [("ps", br)])
                bi = ps_alloc()
                MM(psb[bi][:], wg_bf[:, 4 + j, :], xc_bf[:, j, :], True, True, ["wg_bf", ("xc", j)], [("ps", bi)])
                ACT(tr[s][:], psb[br][:], AF.Tanh, [("ps", br), "dv_hb"], [("tr", s)], bias=dcol(DV_HBRG + j), scale=0.5)
                ps_free(br)
                ACT(ti[s][:], psb[bi][:], AF.Tanh, [("ps", bi), "dv_hb"], [("ti", s)], bias=dcol(DV_HBIG + j), scale=0.5)
                ps_free(bi)
                ACT(a_t[s][:], tr[s][:], AF.Exp, [("tr", s), "dv_cl"], [("a", s)], bias=dcol(DV_HCL + j), scale=dcol(DV_HCL + j))
                TT("pool", w1[s][:], a_t[s][:], a_t[s][:], ALU.mult, [("a", s)], [("w1", s)])

            def restR1(j):
                s = j
                ACT(w1[s][:], w1[s][:], AF.Relu, [("w1", s), "cexp"], [("w1", s)], bias=cexp[:, 0:1], scale=-1.0)

            def restR2(j):
                s = j
                ACT(w1[s][:], w1[s][:], AF.Ln, [("w1", s), "cexp"], [("w1", s)], bias=cexp[:, 1:2])
                ACT(w1[s][:], w1[s][:], AF.Exp, [("w1", s)], [("w1", s)], scale=0.5)

            def restR3(j):
                s = j
                s2 = j % 2
                STT(w2[s2][:], ti[s][:], 1.0, xc_bf[:, j, :], ALU.add, ALU.mult, [("ti", s), ("xc", j)], [("w2", s2)])
                STT(w2[s2][:], w2[s2][:], 0.5, w1[s][:], ALU.mult, ALU.mult, [("w2", s2), ("w1", s)], [("w2", s2)])
                R.add("dve", lambda e: e.tensor_tensor_scan(out=h_t[:, j, :], data0=a_t[s][:], data1=w2[s2][:],
                                                            initial=hcar[:, j:j + 1], op0=ALU.mult, op1=ALU.add),
                      [("a", s), ("w2", s2), ("hcar", j)], [("h", j)])
                COPY("pool", hcar[:, j:j + 1], h_t[:, j, NT - 1:NT], [("h", j)], [("hcar", j)])
                TT("pool", xc_bf[:, j, :], h_t[:, j, :], h_t[:, j, :], ALU.mult, [("h", j)], [("xc", j)])

            def gate(e, gi):
                b = inproj_fm(e)
                s = gi % 2
                ACT(tg[s][:], psb[b][:], AF.Tanh, [("ps", b)], [("tg", s)], scale=0.5)
                STT(sg2[par][:, gi, :], tg[s][:], 1.0, psb[b][:], ALU.add, ALU.mult, [("tg", s), ("ps", b)], [("sg2", 0, gi)])
                ps_free(b)

            def qk_in(e, idx):
                b = inproj_fm(e)
                ACT(sqq[:, idx, :], psb[b][:], AF.Square, [("ps", b)], [("sqq", idx)])
                pending.append(lambda: qk_stat(b, idx))
                pending.append(lambda: qk_norm(b, idx))
                if idx < 3:
                    pend_rope.append(lambda: qk_rope(idx))

            def qk_stat(b, idx):
                s = idx % 2
                b2 = ps_alloc()
                MM(psb[b2][:], ones_bd, sqq[:, idx, :], True, True, ["cm_bf", ("sqq", idx)], [("ps", b2)])
                RSQ(t1b[s][:], psb[b2][:], 1.0 / 64, [("ps", b2)], [("t1b", s)])
                ps_free(b2)

            def qk_norm(b, idx):
                s = idx % 2
                gcol = [V_QG, V_QG, V_KG, V_XQG, V_XQG][idx]
                if idx < 3:
                    dst, dk = qn_bf[:, idx, :], ("qn", idx)
                else:
                    dst, dk = qxn_bf[:, idx - 3, :], ("qxn", idx - 3)
                STT(dst, psb[b][:], vcol(gcol), t1b[s][:], ALU.mult, ALU.mult, [("ps", b), "vec", ("t1b", s)], [dk])
                ps_free(b)

            def qk_rope(idx):
                b = ps_alloc()
                MM(psb[b][:], pmat, qn_bf[:, idx, :], True, True, ["cm_bf", ("qn", idx)], [("ps", b)])
                TT("pool", t2b[:], qn_bf[:, idx, :], cs_c[:], ALU.mult, [("qn", idx), "cs_c"], ["t2b"])
                TT("dve", t3b[:], psb[b][:], cs_s[:], ALU.mult, [("ps", b), "cs_s"], ["t3b"])
                ps_free(b)
                if idx < 2:
                    TT("pool", qrot[:, idx, :], t2b[:], t3b[:], ALU.add, ["t2b", "t3b"], [("qrot", idx)])
                else:
                    TT("pool", kT[par][:], t2b[:], t3b[:], ALU.add, ["t2b", "t3b"], [("kT", par)])

            def v_in():
                b = ps_alloc()
                for t in range(4):
                    for k in range(8):
                        MM(psb[b][:, t * 128:(t + 1) * 128], xT[par][:, k, t * 128:(t + 1) * 128], win_bf[:, k, 11 * 128:12 * 128],
                           k == 0, k == 7, [("win", k, 1), ("xT", 0, t)], [("ps", b)])
                COPY("dve", v_bf[par][:].rearrange("p a b -> p (a b)"), psb[b][:], [("ps", b)], [("v", par)])
                ps_free(b)

            def nt(t):
                if c + 1 < NCH:
                    norm_tile(c + 1, t)

            late = carry["late"]
            carry["late"] = []

            def run_late(n):
                for _ in range(n):
                    if late:
                        late.pop(0)()

            qk_in(8, 0)
            qk_in(9, 1)
            qk_in(10, 2)
            v_in(); run_late(1)
            qk_in(14, 3); run_pending(1); run_late(1)
            qk_in(15, 4); run_pending(2); run_late(1)
            for j in range(4):
                grpA_u(j)
                run_pending(2)
                run_late(1)
            run_late(len(late))
            while pending and len(pending) > 8:
                run_pending(1)
            for gi, e in enumerate([4, 5, 6, 7, 12, 13, 16, 17]):
                gate(e, gi)
                run_pending(1)
                if pend_rope:
                    pend_rope.pop(0)()
            while pending:
                run_pending(1)


            def normA():
                b = ps_alloc()
                for j in range(4):
                    MM(psb[b][:], ones_f, xc_bf[:, j, :], j == 0, j == 3, ["cm_bf", ("xc", j)], [("ps", b)])
                RSQ(rstdg[:, 0, :], psb[b][:], 1.0 / 512, [("ps", b)], [("rstdg", 0)])
                ps_free(b)

            def make_yT(kk):
                if kk < 4:
                    src, skey, gidx = h_t[:, kk, :], ("h", kk), 0
                elif kk < 6:
                    src, skey, gidx = yb_bf[:, kk - 4, :], ("yb", kk - 4), 1
                else:
                    src, skey, gidx = yc_bf[:, kk - 6, :], ("yc", kk - 6), 2
                TT("dve", ygt[0][:], src, sg2[par][:, kk, :], ALU.mult, [skey, ("sg2", 0, kk)], [("ygt", 0)])
                TT("dve", yT[:, kk, :], ygt[0][:], rstdg[:, gidx, :], ALU.mult, [("ygt", 0), ("rstdg", gidx)], [("yT", kk)])

            extras = [lambda: restR1(0), lambda: (restR1(1), nt(0)), lambda: restR2(0), lambda: (restR1(2), nt(1)), lambda: restR2(1),
                      lambda: (restR3(0), nt(2)), lambda: restR1(3), lambda: (restR2(2), nt(3)), lambda: restR3(1), lambda: restR2(3),
                      lambda: restR3(2), lambda: restR3(3), normA,
                      lambda: make_yT(0), lambda: make_yT(1), lambda: make_yT(2), lambda: make_yT(3)]

            units = [("s", n, h) for n in range(4) for h in range(2)]
            units += [("m", jx, qh, half) for jx in range(2) for qh in range(2) for half in range(2)]
            ust = {}

            def swa_sel(n, kb, hs):
                if kb == 1:
                    return (kT[par][hs, n * 128:(n + 1) * 128], ("kT", par), v_bf[par][:, n, hs], ("v", par))
                if n > 0:
                    return (kT[par][hs, (n - 1) * 128:n * 128], ("kT", par), v_bf[par][:, n - 1, hs], ("v", par))
                return (kT[1 - par][hs, 384:512], ("kT", 1 - par), v_bf[1 - par][:, 3, hs], ("v", 1 - par))

            def st1(u):
                un = units[u]
                bs = ps_alloc()
                ust[u] = bs
                if un[0] == "s":
                    _, n, h = un
                    hs = slice(h * 64, (h + 1) * 64)
                    first_blk = (c == 0 and n == 0)
                    if h == 0:
                        ust[("bp", n)] = ps_alloc()
                    kbs = [1] if first_blk else [0, 1]
                    lo = 256 if first_blk else 0
                    first_mm = True
                    for kb in kbs:
                        kap, kkey, _, _ = swa_sel(n, kb, hs)
                        for g in range(2):
                            MM(psb[bs][:, kb * 256 + g * 128: kb * 256 + (g + 1) * 128], kap, qrot[hs, g, n * 128:(n + 1) * 128],
                               first_mm, False, [kkey, ("qrot", g)], [("ps", bs)])
                            first_mm = False
                    MM(psb[bs][:, lo:512], ident, mask_bf[:, lo:512], False, True, ["cm_bf", "mask_bf"], [("ps", bs)])
                else:
                    _, jx, qh, half = un
                    hs = slice(half * 64, (half + 1) * 64)
                    if half == 0:
                        ust[("bn", jx, qh)] = ps_alloc()
                    for mb in range(2):
                        MM(psb[bs][:, mb * 256:(mb + 1) * 256], kmT[hs, jx, mb * 128:(mb + 1) * 128],
                           qxn_bf[hs, jx, qh * 256:(qh + 1) * 256], mb == 0, mb == 1, [("kmT", jx), ("qxn", jx)], [("ps", bs)])

            def st2(u):
                un = units[u]
                bs = ust[u]
                s3 = u % 5
                lo = 256 if (un[0] == "s" and c == 0 and un[1] == 0) else 0
                ACT(PT[s3][:, lo:512], psb[bs][:, lo:512], AF.Exp, [("ps", bs)], [("PT", s3)], scale=0.125)
                ps_free(bs)

            def st3(u):
                un = units[u]
                s3 = u % 5
                if un[0] == "s":
                    _, n, h = un
                    hs = slice(h * 64, (h + 1) * 64)
                    first_blk = (c == 0 and n == 0)
                    kbs = [1] if first_blk else [0, 1]
                    bp = ust[("bp", n)]
                    for g in range(2):
                        for kb in kbs:
                            _, _, vap, vkey = swa_sel(n, kb, hs)
                            rhs = PT[s3][:, kb * 256 + g * 128: kb * 256 + (g + 1) * 128]
                            MM(psb[bp][hs, g * 128:(g + 1) * 128], vap, rhs, kb == kbs[0], kb == 1, [vkey, ("PT", s3)], [("ps", bp)])
                    for g in range(2):
                        for kb in kbs:
                            rhs = PT[s3][:, kb * 256 + g * 128: kb * 256 + (g + 1) * 128]
                            MM(psb[bp][hs, 256 + g * 128: 256 + (g + 1) * 128], ones_f[:, 0:64], rhs, kb == kbs[0], kb == 1,
                               ["cm_bf", ("PT", s3)], [("ps", bp)])
                    if h == 1:
                        rs_ = n % 2
                        for g in range(2):
                            ACT(rden[rs_][:, g * 128:(g + 1) * 128], psb[bp][:, 256 + g * 128: 256 + (g + 1) * 128], AF.Ln,
                                [("ps", bp), "dv_es"], [("rden", rs_)], bias=dcol(DV_ESINK + g))
                        ACT(rden[rs_][:], rden[rs_][:], AF.Exp, [("rden", rs_)], [("rden", rs_)], scale=-1.0)
                        for g in range(2):
                            TT("dve", yb_bf[:, g, n * 128:(n + 1) * 128], psb[bp][:, g * 128:(g + 1) * 128],
                               rden[rs_][:, g * 128:(g + 1) * 128], ALU.mult, [("ps", bp), ("rden", rs_)], [("yb", g)])
                        ps_free(bp)
                else:
                    _, jx, qh, half = un
                    hh = 2 * jx + half
                    hs = slice(half * 64, (half + 1) * 64)
                    bnd = ust[("bn", jx, qh)]
                    for mb in range(2):
                        MM(psb[bnd][hs, 0:256], vm_bf[:, mb, hh * 64:(hh + 1) * 64], PT[s3][:, mb * 256:(mb + 1) * 256], mb == 0, mb == 1,
                           [("vm", mb), ("PT", s3)], [("ps", bnd)])
                    for mb in range(2):
                        MM(psb[bnd][hs, 256:512], ones_f[:, 0:64], PT[s3][:, mb * 256:(mb + 1) * 256], mb == 0, mb == 1,
                           ["cm_bf", ("PT", s3)], [("ps", bnd)])
                    if half == 1:
                        qs = slice(qh * 256, (qh + 1) * 256)
                        ACT(rdenc[:, qs], psb[bnd][:, 256:512], AF.Ln, [("ps", bnd)], [("rdenc", qh)])
                        ACT(rdenc[:, qs], rdenc[:, qs], AF.Exp, [("rdenc", qh)], [("rdenc", qh)], scale=-1.0)
                        TT("dve", yc_bf[:, jx, qs], psb[bnd][:, 0:256], rdenc[:, qs], ALU.mult, [("ps", bnd), ("rdenc", qh)], [("yc", jx)])
                        ps_free(bnd)

            NU = len(units)
            oproj_prev = carry["oproj"]
            carry["oproj"] = []
            for i in range(NU + 5):
                if i < NU:
                    st1(i)
                if 0 <= i - 1 < NU:
                    st2(i - 1)
                if 0 <= i - 5 < NU:
                    st3(i - 5)
                if extras:
                    extras.pop(0)()
                if len(oproj_prev) > 2 and i >= 2:
                    oproj_prev.pop(0)()
                if i in (3, 6, 9, 12) and c + 1 < NCH:
                    transpose_tile(c + 1, (i - 3) // 3)
                if i == 11:
                    if c + 1 < NCH:
                        cs_load(c + 1)
                    while oproj_prev:
                        oproj_prev.pop(0)()
            assert not oproj_prev
            while extras:
                extras.pop(0)()

            def normBC(gidx, ysrc, ykey):
                for jj in range(2):
                    TT("pool", PT[jj][:], ysrc[:, jj, :], ysrc[:, jj, :], ALU.mult, [(ykey, jj)], [("PT", jj)])
                b = ps_alloc()
                for jj in range(2):
                    MM(psb[b][:], ones_f, PT[jj][:], jj == 0, jj == 1, ["cm_bf", ("PT", jj)], [("ps", b)])
                RSQ(rstdg[:, gidx, :], psb[b][:], 1.0 / 256, [("ps", b)], [("rstdg", gidx)])
                ps_free(b)

            def oproj_grp(t, dh):
                T = 4 * c + t
                sl = T % 2
                okey = "ot%d" % sl
                if dh == 0:
                    DMA(ot[sl][:], X[T * 128:(T + 1) * 128, :], [], [okey], okey)
                b = ps_alloc()
                for kk in range(8):
                    MM(psb[b][:], yT[:, kk, t * 128:(t + 1) * 128], wout_bf[:, kk, dh * 512:(dh + 1) * 512], kk == 0, kk == 7,
                       [("yT", kk), ("wout", kk)], [("ps", b)])
                TT("dve", ot[sl][:, dh * 512:(dh + 1) * 512], psb[b][:], ot[sl][:, dh * 512:(dh + 1) * 512], ALU.add,
                   [("ps", b), okey], [okey])
                ps_free(b)
                if dh == 1:
                    DMA(OUT[T * 128:(T + 1) * 128, :], ot[sl][:], [okey], [okey], okey)

            carry["late"] = [lambda: normBC(1, yb_bf, "yb"), lambda: normBC(2, yc_bf, "yc"),
                             lambda: make_yT(4), lambda: make_yT(5), lambda: make_yT(6), lambda: make_yT(7)]
            carry["oproj"] = [(lambda t=t, dh=dh: oproj_grp(t, dh)) for t in range(4) for dh in range(2)]

        side = mem_stages + [lambda: norm_tile(0, 0), lambda: norm_tile(0, 1), lambda: norm_tile(0, 2), lambda: norm_tile(0, 3)]
        side.append(lambda: transposes(0))
        for i, piece in enumerate(win_pieces):
            piece()
            if side:
                side.pop(0)()
        while side:
            side.pop(0)()
        carry["late"] = [(lambda k=k: wout_piece(k)) for k in range(8)]
        for c in range(NCH):
            chunk(c)
        for fn in carry["late"]:
            fn()
        for fn in carry["oproj"]:
            fn()

        dma_keys = R.finalize()
        sems = {}
        for e in ("pe", "act", "dve", "pool"):
            sems[e] = es.enter_context(nc.semaphore("s_" + e))
        for k in dma_keys:
            sems[("dma", k)] = es.enter_context(nc.semaphore("d_" + k))
        ops = R.ops

        def emit_engine(ename, eng):
            waited = {}
            for op in ops:
                if op["eng"] != ename:
                    continue
                for d in sorted(op["deps"]):
                    dop = ops[d]
                    if dop["dma_key"] is not None:
                        key = ("dma", dop["dma_key"])
                        val = R.dma_total[dop["dma_key"]] if dop["dma_key"] in R.group_keys else dop["val"]
                    else:
                        key = dop["eng"]
                        val = dop["val"]
                    if waited.get(key, 0) < val:
                        eng.wait_ge(sems[key], val)
                        waited[key] = val
                ins = op["emit"](eng)
                if op["dma_key"] is not None:
                    ins.then_inc(sems[("dma", op["dma_key"])], 16)
                elif op["signal"]:
                    ins.then_inc(sems[ename], 1)
            if ename == "sp":
                for k in ("ot0", "ot1"):
                    eng.wait_ge(sems[("dma", k)], R.dma_total[k])

        block = es.enter_context(nc.Block())

        @block.tensor
        def _(e):
            emit_engine("pe", e)

        @block.scalar
        def _(e):
            emit_engine("act", e)

        @block.vector
        def _(e):
            emit_engine("dve", e)

        @block.gpsimd
        def _(e):
            emit_engine("pool", e)

        @block.sync
        def _(e):
            emit_engine("sp", e)
    return nc


def _host_consts():
    ident = np.eye(128, dtype=np.float32)
    ones_bd = np.zeros((128, 128), np.float32)
    ones_bd[0:64, 0:64] = 1.0
    ones_bd[64:128, 64:128] = 1.0
    ones_f = np.ones((128, 128), np.float32)
    pm = np.zeros((128, 128), np.float32)
    for base in (0, 64):
        for i in range(8):
            pm[base + i + 8, base + i] = -1.0
            pm[base + i, base + i + 8] = 1.0
    cmat = np.concatenate([ident, ones_bd, ones_f, pm], axis=1)
    kk = np.arange(128)[:, None]
    qq = np.arange(128)[None, :]
    prev = np.where(kk > qq, 0.0, -30000.0).astype(np.float32)
    cur = np.where(kk <= qq, 0.0, -30000.0).astype(np.float32)
    mask = np.concatenate([prev, prev, cur, cur], axis=1)
    pos = np.arange(S, dtype=np.float32)
    inv_freq = (np.float32(500000.0) ** (-(np.arange(0, 16, 2, dtype=np.float32) / np.float32(16.0)))).astype(np.float32)
    ang = pos[None, :] * inv_freq[:, None]
    cos_t = np.ones((128, S), np.float32)
    sin_t = np.zeros((128, S), np.float32)
    for base in (0, 64):
        for i in range(16):
            cos_t[base + i] = np.cos(ang[i % 8])
            sin_t[base + i] = np.sin(ang[i % 8])
    return cmat, mask, cos_t, sin_t


def kernel(x, mem, norm_g, mem_norm_g, w_in, conv_w, conv_b, w_rg, b_rg, w_ig, b_ig, lru_lambda,
           q_norm_g, k_norm_g, sinks, w_mem_kv, xq_norm_g, xk_norm_g, out_norm_g, w_out):
    f = lambda a: np.ascontiguousarray(np.asarray(a, dtype=np.float32))
    x, mem = f(x), f(mem)
    w_in0, w_out0, w_mem0 = f(w_in)[0], f(w_out)[0], f(w_mem_kv)[0]
    cols = []
    cols += list(range(0, 1024))
    sq0 = 1024
    cols += list(range(sq0 + 0, sq0 + 64)) + list(range(sq0 + 128, sq0 + 192))
    cols += list(range(sq0 + 64, sq0 + 128)) + list(range(sq0 + 192, sq0 + 256))
    cols += list(range(1280, 1408))
    cols += list(range(1408, 1536))
    g0 = 1536
    cols += list(range(g0 + 0, g0 + 64)) + list(range(g0 + 128, g0 + 192))
    cols += list(range(g0 + 64, g0 + 128)) + list(range(g0 + 192, g0 + 256))
    cols += list(range(1792, 2304))
    cols = np.array(cols)
    w_in_p = np.ascontiguousarray(w_in0[:, cols])
    rows = list(range(0, 512))
    rows += list(range(512, 576)) + list(range(640, 704))
    rows += list(range(576, 640)) + list(range(704, 768))
    rows += list(range(768, 1024))
    rows = np.array(rows)
    w_out_p = np.ascontiguousarray(w_out0[rows, :])
    vecs = np.zeros((128, NV), np.float32)
    vecs[:, V_NORMG:V_NORMG + 8] = f(norm_g)[0].reshape(8, 128).T
    vecs[:, V_MEMG:V_MEMG + 8] = f(mem_norm_g)[0].reshape(8, 128).T
    vecs[:, V_OUTG:V_OUTG + 8] = f(out_norm_g)[0][rows].reshape(8, 128).T
    cw = f(conv_w)[0]
    for j in range(4):
        for tap in range(4):
            vecs[:, V_CONVW + j * 4 + tap] = cw[tap, j * 128:(j + 1) * 128]
    vecs[:, V_CONVB:V_CONVB + 4] = f(conv_b)[0].reshape(4, 128).T
    vecs[:, V_BRG:V_BRG + 4] = f(b_rg)[0].reshape(4, 128).T
    vecs[:, V_BIG:V_BIG + 4] = f(b_ig)[0].reshape(4, 128).T
    vecs[:, V_LAM:V_LAM + 4] = f(lru_lambda)[0].reshape(4, 128).T
    vecs[:, V_QG] = np.tile(f(q_norm_g)[0], 2)
    vecs[:, V_KG] = np.tile(f(k_norm_g)[0], 2)
    vecs[:, V_XQG] = np.tile(f(xq_norm_g)[0], 2)
    vecs[:, V_XKG] = np.tile(f(xk_norm_g)[0], 2)
    sk = f(sinks)[0]
    for g in range(2):
        vecs[0:64, V_SINK + g] = sk[g]
        vecs[64:128, V_SINK + g] = sk[2 + g]
    wg = np.zeros((128, 8, 128), np.float32)
    wr, wi = f(w_rg)[0], f(w_ig)[0]
    for j in range(4):
        for l in range(2):
            wg[l * 64:(l + 1) * 64, j, l * 64:(l + 1) * 64] = wr[2 * j + l]
            wg[l * 64:(l + 1) * 64, 4 + j, l * 64:(l + 1) * 64] = wi[2 * j + l]
    wg = np.ascontiguousarray(wg.reshape(128, 1024))
    cmat, mask, cos_t, sin_t = _host_consts()

    nc = bass.Bass("TRN2", target_bir_lowering=False)
    build_program(nc)
    in_maps = []
    for b in range(8):
        in_maps.append({
            "x": x[b], "mem": mem[b], "w_in_p": w_in_p, "w_out_p": w_out_p, "w_mem": w_mem0,
            "vecs": vecs, "wg_bd": wg, "cmat": cmat, "mask": mask, "cos_t": cos_t, "sin_t": sin_t,
        })
    res = run_bass_kernel_spmd(nc, in_maps, core_ids=list(range(8)))
    out = np.stack([np.asarray(r["out"], dtype=np.float32) for r in res.results], axis=0)
    return out
```
